# Optimizing a Trainium2 kernel written in Bass

```python
import jax
import jax.numpy as jnp
from jax import lax
import numpy as np

D_MODEL = 1024
BATCH = 8
SEQ = 4096
DEPTH = 1

HEAD_DIM = 64
RWKV_WIDTH = D_MODEL // 2
RWKV_HEADS = RWKV_WIDTH // HEAD_DIM
D_DECAY_LORA = 64
D_AAA_LORA = 64
D_GATE_LORA = 128
RWKV_COLS = 3 * RWKV_WIDTH + D_DECAY_LORA + D_AAA_LORA + D_GATE_LORA
NSA_WIDTH = D_MODEL - RWKV_WIDTH
NSA_Q_HEADS = NSA_WIDTH // HEAD_DIM
NSA_KV_HEADS = 2
NSA_HPG = NSA_Q_HEADS // NSA_KV_HEADS
NSA_KV_WIDTH = NSA_KV_HEADS * HEAD_DIM
N_BRANCH = 3
NSA_COLS = NSA_WIDTH + 6 * NSA_KV_WIDTH + N_BRANCH * NSA_Q_HEADS
IN_COLS = RWKV_COLS + NSA_COLS
CMP_BLOCK = 32
CMP_STRIDE = 16
CMP_HIDDEN = 256
SEL_BLOCK = 64
SEL_TOPK = 16
WINDOW = 512
Q_BLOCK = 64
ROPE_THETA = 10000.0
D_FF = -(-8 * D_MODEL // (3 * 256)) * 256
RMS_EPS = 1e-6
LNX_EPS = 64e-5
FORCE_BONUS = 1e4
NEG_INF = -1e30

kernel_name = 'hybrid_rwkv7_nsa_block'


def rmsnorm(x, w):
    xf = x.astype(jnp.float32)
    y = xf * lax.rsqrt(jnp.mean(xf * xf, axis=-1, keepdims=True) + RMS_EPS)
    return (y * w.astype(jnp.float32)).astype(x.dtype)


def rope(x, positions):
    half = x.shape[-1] // 2
    inv_freq = ROPE_THETA ** (-jnp.arange(half, dtype=jnp.float32) / half)
    ang = positions.astype(jnp.float32)[..., None] * inv_freq
    cos, sin = jnp.cos(ang)[:, :, None, :], jnp.sin(ang)[:, :, None, :]
    x1, x2 = x[..., :half], x[..., half:]
    return jnp.concatenate([x1 * cos - x2 * sin, x2 * cos + x1 * sin], axis=-1)


def token_shift(y):
    return jnp.pad(y, ((0, 0), (1, 0), (0, 0)))[:, :-1]


def rwkv7_group(y, mu, w0, w_lora_up, a0, a_lora_up, g_lora_up, k_k, k_a, r_k, lnx_w, lnx_b):
    B, T, _ = y.shape
    H, N = RWKV_HEADS, HEAD_DIM
    y = y + (token_shift(y) - y) * mu
    splits = [RWKV_WIDTH, 2 * RWKV_WIDTH, 3 * RWKV_WIDTH,
              3 * RWKV_WIDTH + D_DECAY_LORA, 3 * RWKV_WIDTH + D_DECAY_LORA + D_AAA_LORA]
    r, k, v, wd, ad, gd = jnp.split(y, splits, axis=-1)
    w_raw = w0 + jnp.tanh(wd) @ w_lora_up
    decay = jnp.exp(-jnp.exp(-jax.nn.softplus(-w_raw) - 0.5))
    a = jax.nn.sigmoid(a0 + ad @ a_lora_up)
    g = jax.nn.sigmoid(gd) @ g_lora_up
    kk = (k * k_k).reshape(B, T, H, N)
    kk = kk * lax.rsqrt(jnp.maximum(jnp.sum(kk * kk, axis=-1, keepdims=True), 1e-12))
    k = k * (1.0 + (a - 1.0) * k_a)
    r, decay, k, v, a = (z.reshape(B, T, H, N) for z in (r, decay, k, v, a))
    xs = tuple(jnp.moveaxis(z, 1, 0) for z in (r, decay, k, v, kk, a))

    def step(S, inp):
        r_t, w_t, k_t, v_t, kk_t, a_t = inp
        sa = jnp.einsum('bhij,bhj->bhi', S, -kk_t)
        S = (S * w_t[:, :, None, :] + sa[..., None] * (kk_t * a_t)[:, :, None, :]
             + v_t[..., None] * k_t[:, :, None, :])
        return S, jnp.einsum('bhij,bhj->bhi', S, r_t)

    S0 = jnp.zeros((B, H, N, N), jnp.float32)
    _, o = lax.scan(step, S0, xs)
    o = jnp.moveaxis(o, 0, 1)
    mean = jnp.mean(o, axis=-1, keepdims=True)
    var = jnp.mean(jnp.square(o - mean), axis=-1, keepdims=True)
    o = ((o - mean) * lax.rsqrt(var + LNX_EPS)).reshape(B, T, RWKV_WIDTH) * lnx_w + lnx_b
    bonus = jnp.sum(r * k * r_k, axis=-1, keepdims=True) * v
    o = o + bonus.reshape(B, T, RWKV_WIDTH)
    return o * g


def nsa_group(y, positions, cmp_pos_k, cmp_pos_v, cmp_k_w1, cmp_k_w2, cmp_v_w1, cmp_v_w2):
    B, T, _ = y.shape
    G, HPG, D = NSA_KV_HEADS, NSA_HPG, HEAD_DIM
    splits = [NSA_WIDTH + i * NSA_KV_WIDTH for i in range(7)]
    q, kc, vc, ks, vs, kw, vw, gl = jnp.split(y, splits, axis=-1)
    q = rope(q.reshape(B, T, NSA_Q_HEADS, D), positions)
    q = q.reshape(B, T, G, HPG, D).transpose(0, 2, 3, 1, 4)
    gates = jax.nn.sigmoid(gl).reshape(B, T, G, HPG, N_BRANCH).transpose(0, 2, 3, 1, 4)
    kc, ks, kw = (rope(z.reshape(B, T, G, D), positions) for z in (kc, ks, kw))
    vc, vs, vw = (z.reshape(B, T, G, D) for z in (vc, vs, vw))

    n_cmp = (T - CMP_BLOCK) // CMP_STRIDE + 1
    cmp_start = jnp.arange(n_cmp) * CMP_STRIDE
    cmp_idx = cmp_start[:, None] + jnp.arange(CMP_BLOCK)[None, :]

    def compress(z, pos_emb, w1, w2):
        blk = z[:, cmp_idx] + pos_emb[None, None, :, None, :]
        blk = jnp.moveaxis(blk, 3, 1).reshape(B, G, n_cmp, CMP_BLOCK * D)
        return jax.nn.gelu(blk @ w1) @ w2

    k_cmp = compress(kc, cmp_pos_k, cmp_k_w1, cmp_k_w2)
    v_cmp = compress(vc, cmp_pos_v, cmp_v_w1, cmp_v_w2)
    cmp_end = cmp_start + CMP_BLOCK - 1

    n_sb = T // SEL_BLOCK
    n_sel = min(SEL_TOPK, n_sb)
    sb = jnp.arange(n_sb)
    overlap = ((cmp_start[:, None] < (sb[None, :] + 1) * SEL_BLOCK)
               & (cmp_start[:, None] + CMP_BLOCK > sb[None, :] * SEL_BLOCK)).astype(jnp.float32)
    ks_blk = ks.transpose(0, 2, 1, 3).reshape(B, G, n_sb, SEL_BLOCK, D)
    vs_blk = vs.transpose(0, 2, 1, 3).reshape(B, G, n_sb, SEL_BLOCK, D)
    gather = jax.vmap(jax.vmap(lambda blocks, ix: blocks[ix]))

    pad = ((0, 0), (0, 0), (WINDOW, 0), (0, 0))
    kw_pad = jnp.pad(kw.transpose(0, 2, 1, 3), pad)
    vw_pad = jnp.pad(vw.transpose(0, 2, 1, 3), pad)
    scale = HEAD_DIM ** -0.5

    def block(s):
        t = s + jnp.arange(Q_BLOCK)
        qb = lax.dynamic_slice_in_dim(q, s, Q_BLOCK, axis=3)
        gb = lax.dynamic_slice_in_dim(gates, s, Q_BLOCK, axis=3)
        cmask = cmp_end[None, :] <= t[:, None]
        sc = jnp.einsum('bghqd,bgnd->bghqn', qb, k_cmp) * scale
        p_c = jnp.where(cmask, jax.nn.softmax(jnp.where(cmask, sc, NEG_INF), axis=-1), 0.0)
        o_c = jnp.einsum('bghqn,bgnd->bghqd', p_c, v_cmp)
        imp = jnp.einsum('bghqn,nj->bgqj', p_c, overlap)
        tb = t // SEL_BLOCK
        forced = (sb[None, :] == 0) | (sb[None, :] == tb[:, None]) | (sb[None, :] == tb[:, None] - 1)
        imp = jnp.where(sb[None, :] * SEL_BLOCK <= t[:, None], imp + FORCE_BONUS * forced, NEG_INF)
        _, sel = lax.top_k(imp, n_sel)
        k_sel = gather(ks_blk, sel)
        v_sel = gather(vs_blk, sel)
        kpos = sel[..., None] * SEL_BLOCK + jnp.arange(SEL_BLOCK)
        smask = (kpos <= t[None, None, :, None, None])[:, :, None]
        ss = jnp.einsum('bghqd,bgqnld->bghqnl', qb, k_sel) * scale
        ss = jnp.where(smask, ss, NEG_INF).reshape(B, G, HPG, Q_BLOCK, n_sel * SEL_BLOCK)
        p_s = jax.nn.softmax(ss, axis=-1).reshape(B, G, HPG, Q_BLOCK, n_sel, SEL_BLOCK)
        o_s = jnp.einsum('bghqnl,bgqnld->bghqd', p_s, v_sel)
        kwb = lax.dynamic_slice_in_dim(kw_pad, s, WINDOW + Q_BLOCK, axis=2)
        vwb = lax.dynamic_slice_in_dim(vw_pad, s, WINDOW + Q_BLOCK, axis=2)
        kp = s - WINDOW + jnp.arange(WINDOW + Q_BLOCK)
        wmask = (kp[None, :] <= t[:, None]) & (kp[None, :] > t[:, None] - WINDOW) & (kp[None, :] >= 0)
        sw = jnp.einsum('bghqd,bgkd->bghqk', qb, kwb) * scale
        p_w = jax.nn.softmax(jnp.where(wmask, sw, NEG_INF), axis=-1)
        o_w = jnp.einsum('bghqk,bgkd->bghqd', p_w, vwb)
        return gb[..., 0:1] * o_c + gb[..., 1:2] * o_s + gb[..., 2:3] * o_w

    o = lax.map(block, jnp.arange(T // Q_BLOCK) * Q_BLOCK)
    return o.transpose(1, 0, 4, 2, 3, 5).reshape(B, T, NSA_WIDTH)


def setup_inputs(seed: int = 0) -> dict:
    key = jax.random.key(seed)
    ks = jax.random.split(key, 32)
    L = DEPTH

    def nrm(k, shape, scale):
        return jax.random.normal(k, shape, jnp.float32) * scale

    return {
        'x': nrm(ks[0], (BATCH, SEQ, D_MODEL), 1.0),
        'positions': jnp.broadcast_to(jnp.arange(SEQ, dtype=jnp.int32), (BATCH, SEQ)),
        'norm1_w': 1.0 + nrm(ks[1], (L, D_MODEL), 0.02),
        'w_in': nrm(ks[2], (L, D_MODEL, IN_COLS), D_MODEL ** -0.5),
        'mu_rwkv': jax.random.uniform(ks[3], (L, RWKV_COLS), jnp.float32),
        'w0': jax.random.uniform(ks[4], (L, RWKV_WIDTH), jnp.float32, minval=-3.0, maxval=1.0),
        'w_lora_up': nrm(ks[5], (L, D_DECAY_LORA, RWKV_WIDTH), 0.1),
        'a0': nrm(ks[6], (L, RWKV_WIDTH), 0.1),
        'a_lora_up': nrm(ks[7], (L, D_AAA_LORA, RWKV_WIDTH), 0.5 * D_AAA_LORA ** -0.5),
        'g_lora_up': nrm(ks[8], (L, D_GATE_LORA, RWKV_WIDTH), D_GATE_LORA ** -0.5),
        'k_k': 0.85 + nrm(ks[9], (L, RWKV_WIDTH), 0.02),
        'k_a': 1.0 + nrm(ks[10], (L, RWKV_WIDTH), 0.02),
        'r_k': nrm(ks[11], (L, RWKV_HEADS, HEAD_DIM), 0.1),
        'lnx_w': 1.0 + nrm(ks[12], (L, RWKV_WIDTH), 0.02),
        'lnx_b': nrm(ks[13], (L, RWKV_WIDTH), 0.01),
        'cmp_pos_k': nrm(ks[14], (L, CMP_BLOCK, HEAD_DIM), 0.02),
        'cmp_pos_v': nrm(ks[15], (L, CMP_BLOCK, HEAD_DIM), 0.02),
        'cmp_k_w1': nrm(ks[16], (L, CMP_BLOCK * HEAD_DIM, CMP_HIDDEN), (CMP_BLOCK * HEAD_DIM) ** -0.5),
        'cmp_k_w2': nrm(ks[17], (L, CMP_HIDDEN, HEAD_DIM), CMP_HIDDEN ** -0.5),
        'cmp_v_w1': nrm(ks[18], (L, CMP_BLOCK * HEAD_DIM, CMP_HIDDEN), (CMP_BLOCK * HEAD_DIM) ** -0.5),
        'cmp_v_w2': nrm(ks[19], (L, CMP_HIDDEN, HEAD_DIM), CMP_HIDDEN ** -0.5),
        'w_out': nrm(ks[20], (L, D_MODEL, D_MODEL), D_MODEL ** -0.5),
        'norm2_w': 1.0 + nrm(ks[21], (L, D_MODEL), 0.02),
        'ffn_w1': nrm(ks[22], (L, D_MODEL, D_FF), D_MODEL ** -0.5),
        'ffn_w3': nrm(ks[23], (L, D_MODEL, D_FF), D_MODEL ** -0.5),
        'ffn_w2': nrm(ks[24], (L, D_FF, D_MODEL), D_FF ** -0.5),
        'final_norm_w': 1.0 + nrm(ks[25], (D_MODEL,), 0.02),
    }


def reference(x, positions, norm1_w, w_in, mu_rwkv, w0, w_lora_up, a0, a_lora_up, g_lora_up,
              k_k, k_a, r_k, lnx_w, lnx_b, cmp_pos_k, cmp_pos_v, cmp_k_w1, cmp_k_w2,
              cmp_v_w1, cmp_v_w2, w_out, norm2_w, ffn_w1, ffn_w3, ffn_w2, final_norm_w):
    h = x
    for l in range(DEPTH):
        y = (rmsnorm(h, norm1_w[l]) @ w_in[l]).astype(jnp.float32)
        o_rwkv = rwkv7_group(y[..., :RWKV_COLS], mu_rwkv[l], w0[l], w_lora_up[l], a0[l],
                             a_lora_up[l], g_lora_up[l], k_k[l], k_a[l], r_k[l], lnx_w[l], lnx_b[l])
        o_nsa = nsa_group(y[..., RWKV_COLS:], positions, cmp_pos_k[l], cmp_pos_v[l],
                          cmp_k_w1[l], cmp_k_w2[l], cmp_v_w1[l], cmp_v_w2[l])
        o = jnp.concatenate([o_rwkv, o_nsa], axis=-1).astype(h.dtype)
        h = h + o @ w_out[l]
        u = rmsnorm(h, norm2_w[l])
        h = h + (jax.nn.silu(u @ ffn_w1[l]) * (u @ ffn_w3[l])) @ ffn_w2[l]
    return rmsnorm(h, final_norm_w)
```

```python
from contextlib import ExitStack
import math
import numpy as np
import concourse.bass as bass
import concourse.mybir as mybir
from concourse.bass_utils import run_bass_kernel_spmd

F32 = mybir.dt.float32
BF16 = mybir.dt.bfloat16
I32 = mybir.dt.int32
AF = mybir.ActivationFunctionType
ALU = mybir.AluOpType
AX = mybir.AxisListType

NDS = 24
T_SEQ = 4096
NT = 32
DM = 1024
RW = 512
RCOLS = 1792
INC = 3096
DFF = 2816
CDEC = -math.exp(-0.5)
RMS_EPS = 1e-6
LNX_EPS = 64e-5


class Buf:
    __slots__ = ("w", "r", "name")

    def __init__(self, name=""):
        self.w = None
        self.r = []
        self.name = name


class T:
    def __init__(self, h, nslots=1, name=""):
        self.h = h
        self.b = [Buf(f"{name}.{i}") for i in range(nslots)]

    def __getitem__(self, k):
        return self.h[k]


class KB:
    ENG = ("pe", "dve", "act", "pool", "sp")

    def __init__(self, nc):
        self.nc = nc
        self.gs = ExitStack()
        self.es = None
        self.ops = {e: [] for e in self.ENG}
        self.cnt = {e: 0 for e in ("pe", "dve", "act", "pool")}
        self.seen = {e: {} for e in self.ENG}
        self.sem = {}
        for e in ("pe", "dve", "act", "pool"):
            self.sem[e] = self.gs.enter_context(nc.semaphore("s_" + e))
        self.dsem = [self.gs.enter_context(nc.semaphore(f"d{i}")) for i in range(NDS)]
        self.dcnt = [0] * NDS
        self.dq = {"sp": list(range(0, NDS // 2)), "pool": list(range(NDS // 2, NDS))}
        self.dnext = {"sp": 0, "pool": 0}
        self.same_engine_sync = True
        self.same_engine_all = True
        self.ntiles = 0
        self.banks = []
        self.bank_i = 0

    def begin(self):
        self.es = ExitStack()
        self.ops = {e: [] for e in self.ENG}
        self.banks = [self.ps() for _ in range(8)]
        self.set_pools({"all": range(8)})

    def end(self):
        self.wait_all("sp", [(("d", k), self.dcnt[k]) for k in range(NDS) if self.dcnt[k]])
        self.emit()
        self.es.close()
        self.es = None

    def set_pools(self, pools):
        self.pools = {k: list(v) for k, v in pools.items()}
        self.pool_i = {k: 0 for k in pools}

    def psn(self, pool="all"):
        ids = self.pools[pool]
        b = self.banks[ids[self.pool_i[pool] % len(ids)]]
        self.pool_i[pool] += 1
        return b

    def sb(self, shape, dtype=F32, nslots=1, name=None):
        self.ntiles += 1
        name = f"sb{self.ntiles}_" + (name or "t")
        h = self.es.enter_context(self.nc.sbuf_tensor(name, list(shape), dtype))
        return T(h, nslots, name)

    def ps(self, shape=(128, 512), dtype=F32, name=None):
        self.ntiles += 1
        name = f"ps{self.ntiles}_" + (name or "p")
        h = self.es.enter_context(self.nc.psum_tensor(name, list(shape), dtype))
        return T(h, 1, name)

    def dram(self, name, shape, dtype, kind="Internal"):
        h = self.nc.dram_tensor(name, list(shape), dtype, kind=kind)
        return T(h.ap(), 1, name)

    def _deps(self, eng, reads, writes):
        deps = set()
        for b in reads:
            if b.w is not None:
                deps.add(b.w)
        for b in writes:
            if b.w is not None and (b.w[0] != eng or self.same_engine_all):
                deps.add(b.w)
            for t_ in b.r:
                if t_[0] != eng or self.same_engine_all:
                    deps.add(t_)
        seen = self.seen[eng]
        best = {}
        for (k, n) in deps:
            if k == eng and (eng == "pe" or not self.same_engine_sync):
                continue
            if seen.get(k, 0) < n:
                best[k] = max(best.get(k, 0), n)
        waits = []
        for k, n in best.items():
            seen[k] = n
            waits.append((k, n))
        return waits

    @staticmethod
    def _mark(tok, reads, writes):
        for b in reads:
            b.r.append(tok)
        for b in writes:
            b.w = tok
            b.r = []

    @staticmethod
    def _bufs(lst):
        out = []
        for x in lst:
            if isinstance(x, T):
                out.extend(x.b)
            elif isinstance(x, Buf):
                out.append(x)
            else:
                out.extend(KB._bufs(x))
        return out

    def op(self, eng, fn, r=(), w=()):
        reads, writes = self._bufs(r), self._bufs(w)
        waits = self._deps(eng, reads, writes)
        self.cnt[eng] += 1
        tok = (eng, self.cnt[eng])
        self._mark(tok, reads, writes)
        self.ops[eng].append((waits, fn, ("c", eng)))
        return tok

    def dma(self, q, out, in_, r=(), w=(), **kw):
        reads, writes = self._bufs(r), self._bufs(w)
        k = self.dq[q][self.dnext[q]]
        self.dnext[q] = (self.dnext[q] + 1) % len(self.dq[q])
        waits = self._deps(q, reads, writes)
        prev = self.dcnt[k]
        if prev and self.seen[q].get(("d", k), 0) < prev:
            self.seen[q][("d", k)] = prev
            waits.append((("d", k), prev))
        self.dcnt[k] += 16
        tok = (("d", k), self.dcnt[k])
        self._mark(tok, reads, writes)
        self.ops[q].append((waits, lambda e: e.dma_start(out=out, in_=in_, **kw), ("d", k)))
        return tok

    def wait_all(self, q, toks):
        waits = []
        for (k, n) in toks:
            if self.seen[q].get(k, 0) < n:
                self.seen[q][k] = n
                waits.append((k, n))
        self.ops[q].append((waits, None, None))

    def _semof(self, k):
        if isinstance(k, tuple):
            return self.dsem[k[1]]
        return self.sem[k]

    def _run(self, e, name):
        for waits, fn, kind in self.ops[name]:
            for (k, n) in waits:
                e.wait_ge(self._semof(k), n)
            if fn is None:
                continue
            ins = fn(e)
            if kind[0] == "c":
                ins.then_inc(self.sem[kind[1]], 1)
            else:
                ins.then_inc(self.dsem[kind[1]], 16)

    def emit(self):
        with self.nc.Block() as block:
            @block.tensor
            def _(e):
                self._run(e, "pe")

            @block.vector
            def _(e):
                self._run(e, "dve")

            @block.scalar
            def _(e):
                self._run(e, "act")

            @block.gpsimd
            def _(e):
                self._run(e, "pool")

            @block.sync
            def _(e):
                self._run(e, "sp")

    def tt(self, eng, out, a, b, op, r, w):
        return self.op(eng, lambda e: e.tensor_tensor(out, a, b, op), r, w)

    def ts(self, eng, out, a, s1, s2, op0, op1, r, w):
        if s2 is None:
            return self.op(eng, lambda e: e.tensor_scalar(out, a, s1, None, op0), r, w)
        return self.op(eng, lambda e: e.tensor_scalar(out, a, s1, s2, op0, op1), r, w)

    def stt(self, out, a, s, b, op0, op1, r, w, eng="dve"):
        return self.op(eng, lambda e: e.scalar_tensor_tensor(out, a, s, b, op0, op1), r, w)

    def act(self, out, in_, func, r, w, **kw):
        return self.op("act", lambda e: e.activation(out, in_, func, **kw), r, w)

    def cp(self, eng, out, in_, r, w):
        if eng == "act":
            return self.op("act", lambda e: e.copy(out, in_), r, w)
        return self.op(eng, lambda e: e.tensor_copy(out, in_), r, w)

    def mm(self, out, lhsT, rhs, start, stop, r, w):
        return self.op("pe", lambda e: e.matmul(out, lhsT, rhs, start=start, stop=stop), r, w)

    def tr(self, out, in_, ident, r, w):
        return self.op("pe", lambda e: e.transpose(out, in_, ident), r, w)

    def red(self, eng, out, in_, op, r, w):
        return self.op(eng, lambda e: e.tensor_reduce(out, in_, AX.X, op), r, w)

    def memset(self, eng, ap, val, w):
        return self.op(eng, lambda e: e.memset(ap, val), (), w)

    def asel(self, out, in_, pattern, cmp, fill, base, cm, r, w):
        return self.op("pool", lambda e: e.affine_select(out=out, in_=in_, pattern=pattern, compare_op=cmp,
                                                         fill=fill, base=base, channel_multiplier=cm), r, w)


CUT = 10 ** 9
PIPE_W = [1, 1, 1]
M_LIST = None


class Cut(Exception):
    pass


def ck(n):
    if n > CUT:
        raise Cut()


def v3(ap, h):
    return ap.rearrange("p (h d) -> p h d", h=h)


def bc_last(ap2, n):
    p, h = ap2.shape
    return ap2.unsqueeze(2).to_broadcast([p, h, n])


def bc_mid(ap2, n):
    p, d = ap2.shape
    return ap2.unsqueeze(1).to_broadcast([p, n, d])


def phase0(kb, D, S, dbg_tiles=NT):
    kb.begin()
    ident = kb.sb([128, 128], name="ident")
    kb.memset("pool", ident[:], 0.0, [ident])
    kb.asel(ident[:], ident[:], [[-1, 128]], ALU.not_equal, 1.0, 0, 1, [ident], [ident])

    WIN = kb.sb([128, 8, INC], BF16, nslots=8, name="WIN")
    for k in range(8):
        kb.dma("pool", WIN[:, k, :], D["w_in"][k * 128:(k + 1) * 128, :], w=[WIN.b[k]])

    def bload(name, n, q="sp"):
        t = kb.sb([128, n], name="b_" + name)
        kb.dma(q, t[:], D[name][0:1, :].partition_broadcast(128), w=[t])
        return t

    n1w = bload("norm1_w", DM)
    mu = bload("mu", RCOLS)
    w0 = bload("w0", RW)
    a0 = bload("a0", RW)
    kkw = bload("k_k", RW)
    kaw = bload("k_a", RW)
    rkw = bload("r_k", RW)
    lnw = bload("lnx_w", RW)
    lnb = bload("lnx_b", RW)
    wlora = kb.sb([128, RW], BF16, name="wlora")
    kb.dma("pool", wlora[0:64, :], D["w_lora_up"][:, :], w=[wlora])
    kb.dma("pool", wlora[64:128, :], D["a_lora_up"][:, :], w=[wlora])
    glora = kb.sb([128, RW], BF16, name="glora")
    kb.dma("pool", glora[:], D["g_lora_up"][:, :], w=[glora])

    def mask(free, pattern, cm, cmp, init, fill, name):
        t = kb.sb([128] + free, name=name)
        kb.memset("pool", t[0:64], init, [t])
        kb.asel(t[0:64], t[0:64], pattern, cmp, fill, 0, cm, [t], [t])
        kb.dma("sp", t[64:128], t[0:64], r=[t], w=[t])
        return t

    SU = mask([8, 64], [[0, 8], [1, 64]], -1, ALU.is_gt, 1.0, 0.0, "SU")
    IU = mask([8, 64], [[0, 8], [1, 64]], -1, ALU.is_ge, 1.0, 0.0, "IU")
    SL = mask([8, 64], [[0, 8], [-1, 64]], 1, ALU.is_gt, 1.0, 0.0, "SL")
    EYE = mask([8, 64], [[0, 8], [-1, 64]], 1, ALU.not_equal, 0.0, 1.0, "EYE")
    LIN = mask([64], [[1, 64]], -1, ALU.is_ge, 1.0, 0.0, "LIN")
    ONES = kb.sb([128, 64], name="ONES")
    kb.memset("pool", ONES[:], 1.0, [ONES])
    LBD = kb.sb([128, 128], name="LBD")
    kb.memset("pool", LBD[:], 1.0, [LBD])
    kb.asel(LBD[:], LBD[:], [[1, 128]], ALU.is_ge, 0.0, 0, -1, [LBD], [LBD])
    kb.memset("pool", LBD[0:64, 64:128], 0.0, [LBD])
    OBD = kb.sb([128, 128], name="OBD")
    kb.memset("pool", OBD[:], 0.0, [OBD])
    kb.memset("pool", OBD[0:64, 0:64], 1.0, [OBD])
    kb.memset("pool", OBD[64:128, 64:128], 1.0, [OBD])
    NEGH = kb.sb([128, 8], name="NEGH")
    kb.memset("pool", NEGH[:], -0.5, [NEGH])

    posi = kb.sb([128, NT], I32, name="posi")
    kb.dma("sp", posi[:], D["pos"][:, :], w=[posi])
    posf = kb.sb([128, NT], name="posf")
    kb.cp("dve", posf[:], posi[:], [posi], [posf])
    invf = kb.sb([128, 32], name="invf")
    kb.dma("sp", invf[:], D["invf"][:, :], w=[invf])
    X = [kb.sb([128, DM], name=f"x{j}") for j in range(2)]
    xn = kb.sb([128, DM], name="xn")
    COS = kb.sb([128, NT, 32], name="COS")
    SIN = kb.sb([128, NT, 32], name="SIN")
    ang = T(v3(X[0][:], NT), name="ang")
    ang.b = X[0].b
    rf = T(v3(X[1][:], NT), name="rf")
    rf.b = X[1].b
    ri = T(v3(xn[:].bitcast(I32), NT), name="ri")
    ri.b = xn.b
    kb.tt("dve", ang[:], bc_last(posf[:], 32), bc_mid(invf[:], NT), ALU.mult, [posf, invf], [ang])
    for tab, off in ((SIN, 0.0), (COS, 0.25)):
        kb.ts("dve", rf[:], ang[:], 1.0 / (2 * math.pi), off, ALU.mult, ALU.add, [ang], [rf])
        kb.cp("dve", ri[:], rf[:], [rf], [ri])
        kb.cp("dve", tab[:], ri[:], [ri], [tab])
        kb.tt("dve", rf[:], rf[:], tab[:], ALU.subtract, [rf, tab], [rf])
        kb.act(tab[:], rf[:], AF.Sin, [rf], [tab], scale=2 * math.pi)

    ssq = kb.sb([128, 1], name="ssq")
    rstd = kb.sb([128, 1], name="rstd")
    xnT = kb.sb([128, 8, 129], BF16, name="xnT")
    yr = kb.sb([128, RCOLS], name="yr")
    YL2 = [kb.sb([128, RCOLS], name=f"yl{j}") for j in range(2)]
    ro = kb.sb([128, 14, 64], name="ro")
    rt = [kb.sb([128, 8, 32], name=f"rt{j}") for j in range(2)]
    vt = kb.sb([128, 408], name="vt")
    stq = [kb.sb([128, 4, 128], BF16, name="stq0")] * 2
    stk = [kb.sb([128, 4, 128], BF16, name="stk0")] * 2
    stv = [kb.sb([128, 2, 2, 65], BF16, name=f"stv{j}") for j in range(2)]
    stg = [kb.sb([128, 24], name=f"stg{j}") for j in range(2)]
    for j in range(2):
        kb.memset("pool", stv[j][:], 1.0, [stv[j]])
    lo = kb.sb([128, 256], name="lo")
    loT = kb.sb([128, 2, 128], BF16, name="loT")
    W = [kb.sb([128, RW], name=f"w{j}") for j in range(9)]
    sig, a_t, kkn, e3, e4, e5, e6, o_t, pe5 = W
    g_t = kb.sb([128, RW], name="g_t")
    kp = kb.sb([128, RW], name="kp")
    BG2 = [kb.sb([128, RW], name=f"bg{j}") for j in range(2)]
    GW2 = [kb.sb([128, RW], name=f"gw{j}") for j in range(2)]
    sm = [kb.sb([128, 8], name=f"sm{j}") for j in range(4)]
    Bt2 = [kb.sb([128, RW], BF16, name=f"Bt{j}") for j in range(2)]
    Kt2 = [kb.sb([128, RW], BF16, name=f"Kt{j}") for j in range(2)]
    Vb2 = [kb.sb([128, RW], BF16, name=f"Vb{j}") for j in range(2)]
    FM2 = [{q: [kb.sb([128, 8, 64], BF16, name=f"fm{q}{c}_{j}") for c in range(2)] for q in ("AR" if j else "ARBK")}
           for j in range(2)]
    FM2[1]["B"] = FM2[0]["B"]
    FM2[1]["K"] = FM2[0]["K"]
    gC2 = [[kb.sb([128, 8], name=f"gC{c}_{j}") for c in range(2)] for j in range(2)]
    CH_DT = BF16
    Y = [kb.sb([128, 8, 64], CH_DT, name=f"Y{j}") for j in range(2)]
    YT = [kb.sb([128, 8, 64], CH_DT, name=f"YT{j}") for j in range(2)]
    Xi = kb.sb([128, 8, 64], CH_DT, name="Xi")
    Xb2 = [kb.sb([128, 8, 64], BF16, name=f"Xb{j}") for j in range(2)]
    Mkb2 = [kb.sb([128, 8, 64], BF16, name=f"Mkb{j}") for j in range(2)]
    Nbr2 = [kb.sb([128, 8, 64], BF16, name=f"Nbr{j}") for j in range(2)]
    Nkr2 = [kb.sb([128, 8, 64], BF16, name=f"Nkr{j}") for j in range(2)]
    RH = kb.sb([128, 8, 64], BF16, name="RH")
    Ub = kb.sb([128, 8, 64], BF16, name="Ub")
    St = kb.sb([128, 8, 64], name="St")
    Sb = kb.sb([128, 8, 64], BF16, name="Sb")
    kb.memset("pool", St[:], 0.0, [St])
    kb.memset("pool", Sb[:], 0.0, [Sb])
    kb.memset("pool", xnT[:], 0.0, [xnT])

    GR = [(0, 512), (512, 1024), (1024, 1536), (1536, 1792), (1792, 2304), (2304, 2688), (2688, 3096)]

    kb.set_pools({"A": [0, 1], "R": [2, 3, 4, 5], "S": [6], "O": [7]})

    def stageA(i):
        cur, prv = i % 2, (i + 1) % 2
        xt = X[cur]
        kb.dma("sp", xt[:], D["x"][i * 128:(i + 1) * 128, :], w=[xt])
        kb.act(xn[:], xt[:], AF.Square, [xt], [xn, ssq], accum_out=ssq[:])
        kb.ts("dve", rstd[:], ssq[:], 1.0 / DM, RMS_EPS, ALU.mult, ALU.add, [ssq], [rstd])
        kb.tt("pool", rstd[:], rstd[:], NEGH[:, 0:1], ALU.pow, [rstd, NEGH], [rstd])
        kb.stt(xn[:], xt[:], rstd[:, 0:1], n1w[:], ALU.mult, ALU.mult, [xt, rstd, n1w], [xn])
        yield
        for half in range(2):
            ps = kb.psn("A")
            for kk_ in range(4):
                k = half * 4 + kk_
                kb.tr(ps[:, kk_ * 128:(kk_ + 1) * 128], xn[:, k * 128:(k + 1) * 128], ident[:], [xn, ident], [ps])
            if half == 0 and i > 0:
                kb.cp("dve", xnT[:, :, 0:1], xnT[:, :, 128:129], [xnT], [xnT])
            kb.cp("act" if half == 0 else "dve", xnT[:, half * 4:(half + 1) * 4, 1:129], v3(ps[:, :], 4), [ps], [xnT])
        yield
        yl = YL2[i % 2]
        for gi, (ca, cb) in enumerate(GR):
            ps = kb.psn("A")
            n = cb - ca
            for k in range(8):
                kb.mm(ps[:, 0:n], xnT[:, k, 1:129], WIN[:, k, ca:cb], k == 0, k == 7, [xnT, WIN.b[k]], [ps])
            if gi < 4:
                kb.cp("act", yr[:, ca:cb], ps[:, 0:n], [ps], [yr])
                ps2 = kb.psn("A")
                for k in range(8):
                    kb.mm(ps2[:, 0:n], xnT[:, k, 0:128], WIN[:, k, ca:cb], k == 0, k == 7, [xnT, WIN.b[k]], [ps2])
                kb.tt("dve", yl[:, ca:cb], ps2[:, 0:n], yr[:, ca:cb], ALU.subtract, [ps2, yr], [yl])
                kb.tt("pool", yl[:, ca:cb], yl[:, ca:cb], mu[:, ca:cb], ALU.mult, [yl, mu], [yl])
                kb.tt("pool", yl[:, ca:cb], yl[:, ca:cb], yr[:, ca:cb], ALU.add, [yl, yr], [yl])
            elif gi in (4, 5):
                nh = 8 if gi == 4 else 6
                h0 = 0 if gi == 4 else 8
                pv = v3(ps[:, 0:nh * 64], nh)
                x1, x2 = pv[:, :, 0:32], pv[:, :, 32:64]
                cb_ = bc_mid(COS[:, i, :], nh)
                sb_ = bc_mid(SIN[:, i, :], nh)
                t1, t2 = rt[0][:, 0:nh, :], rt[1][:, 0:nh, :]
                kb.tt("dve", t1, x1, cb_, ALU.mult, [ps, COS], [rt[0]])
                kb.tt("dve", t2, x2, sb_, ALU.mult, [ps, SIN], [rt[1]])
                kb.tt("pool", ro[:, h0:h0 + nh, 0:32], t1, t2, ALU.subtract, [rt[0], rt[1]], [ro])
                kb.tt("dve", t1, x2, cb_, ALU.mult, [ps, COS], [rt[0]])
                kb.tt("dve", t2, x1, sb_, ALU.mult, [ps, SIN], [rt[1]])
                kb.tt("pool", ro[:, h0:h0 + nh, 32:64], t1, t2, ALU.add, [rt[0], rt[1]], [ro])
            else:
                kb.cp("act", vt[:], ps[:, 0:408], [ps], [vt])
            yield
        yield
        rof = ro[:].rearrange("p h d -> p (h d)")
        ps = kb.psn("A")
        for p_ in range(4):
            kb.tr(ps[:, p_ * 128:(p_ + 1) * 128], rof[:, p_ * 128:(p_ + 1) * 128], ident[:], [ro, ident], [ps])
        kb.cp("act", stq[cur][:], v3(ps[:, :], 4), [ps], [stq[cur]])
        kb.dma("sp", S["dQT"][:, :, i * 128:(i + 1) * 128], stq[cur][:], r=[stq[cur]], w=[S["dQT"]])
        ps = kb.psn("A")
        for p_ in range(3):
            kb.tr(ps[:, p_ * 128:(p_ + 1) * 128], rof[:, (4 + p_) * 128:(5 + p_) * 128], ident[:], [ro, ident], [ps])
        kb.tr(ps[:, 384:512], vt[:, 0:128], ident[:], [vt, ident], [ps])
        kb.cp("dve", stk[cur][:], v3(ps[:, :], 4), [ps], [stk[cur]])
        kb.dma("sp", S["dKT"][:, :, i * 128:(i + 1) * 128], stk[cur][:], r=[stk[cur]], w=[S["dKT"]])
        kb.cp("pool", stv[cur][:, :, :, 0:64], vt[:, 128:384].rearrange("p (a g d) -> p a g d", a=2, g=2), [vt], [stv[cur]])
        kb.dma("sp", S["dV"][:, i, :, :, :], stv[cur][:], r=[stv[cur]], w=[S["dV"]])
        kb.act(stg[cur][:], vt[:, 384:408], AF.Tanh, [vt], [stg[cur]], scale=0.5)
        kb.act(stg[cur][:], stg[cur][:], AF.Identity, [stg[cur]], [stg[cur]], scale=0.5, bias=0.5)
        kb.dma("sp", S["dG"][:, i, :], stg[cur][:], r=[stg[cur]], w=[S["dG"]])


    def bind(i):
        j = i % 2
        return (BG2[j], GW2[j], Bt2[j], Kt2[j], Vb2[j], FM2[j], gC2[j], Xb2[j], Mkb2[j], Nbr2[j], Nkr2[j])

    def stageR(i):
        cur, prv = i % 2, (i + 1) % 2
        yl = YL2[i % 2]
        BG, GW, Bt, Kt, Vb, FM, gC, Xb, Mkb, Nbr, Nkr = bind(i)
        r_ = yl[:, 0:512]
        k_ = yl[:, 512:1024]
        v_ = yl[:, 1024:1536]
        yield
        kb.act(lo[:, 0:64], yl[:, 1536:1600], AF.Tanh, [yl], [lo])
        kb.cp("pool", lo[:, 64:128], yl[:, 1600:1664], [yl], [lo])
        kb.act(lo[:, 128:256], yl[:, 1664:1792], AF.Tanh, [yl], [lo], scale=0.5)
        kb.act(lo[:, 128:256], lo[:, 128:256], AF.Identity, [lo], [lo], scale=0.5, bias=0.5)
        kb.tt("pool", kkn[:], k_, kkw[:], ALU.mult, [yl, kkw], [kkn])
        kb.act(e3[:], kkn[:], AF.Square, [kkn], [e3])
        kb.red("dve", sm[0][:], v3(e3[:], 8), ALU.add, [e3], [sm[0]])
        kb.ts("dve", sm[0][:], sm[0][:], 1e-12, None, ALU.max, None, [sm[0]], [sm[0]])
        kb.tt("pool", sm[0][:], sm[0][:], NEGH[:], ALU.pow, [sm[0], NEGH], [sm[0]])
        kb.tt("dve", v3(kkn[:], 8), v3(kkn[:], 8), bc_last(sm[0][:], 64), ALU.mult, [kkn, sm[0]], [kkn])
        ps = kb.psn("R")
        for j in range(2):
            kb.tr(ps[:, j * 128:(j + 1) * 128], lo[:, j * 128:(j + 1) * 128], ident[:], [lo, ident], [ps])
        kb.cp("act", loT[:], v3(ps[:, 0:256], 2), [ps], [loT])
        psw = kb.psn("R")
        kb.mm(psw[:, :], loT[0:64, 0, :], wlora[0:64, :], True, True, [loT, wlora], [psw])
        psa = kb.psn("R")
        kb.mm(psa[:, :], loT[64:128, 0, :], wlora[64:128, :], True, True, [loT, wlora], [psa])
        psg = kb.psn("R")
        kb.mm(psg[:, :], loT[:, 1, :], glora[:, :], True, True, [loT, glora], [psg])
        kb.tt("dve", sig[:], psw[:, :], w0[:], ALU.add, [psw, w0], [sig])
        kb.act(sig[:], sig[:], AF.Tanh, [sig], [sig], scale=0.5)
        kb.ts("dve", sig[:], sig[:], 0.5, 0.5, ALU.mult, ALU.add, [sig], [sig])
        kb.tt("dve", a_t[:], psa[:, :], a0[:], ALU.add, [psa, a0], [a_t])
        kb.act(a_t[:], a_t[:], AF.Tanh, [a_t], [a_t], scale=0.5)
        kb.act(a_t[:], a_t[:], AF.Identity, [a_t], [a_t], scale=0.5, bias=0.5)
        kb.cp("act", g_t[:], psg[:, :], [psg], [g_t])
        yield
        kb.stt(kp[:], a_t[:], -1.0, kaw[:], ALU.add, ALU.mult, [a_t, kaw], [kp])
        kb.stt(kp[:], kp[:], 1.0, k_, ALU.add, ALU.mult, [kp, yl], [kp])
        kb.tt("pool", a_t[:], a_t[:], kkn[:], ALU.mult, [a_t, kkn], [a_t])
        beta = a_t
        yield
        psc = kb.psn("R")
        pst = kb.psn("R")
        kb.mm(psc[:, :], LBD[:, :], sig[:, :], True, True, [LBD, sig], [psc])
        kb.mm(pst[:, :], OBD[:, :], sig[:, :], True, True, [OBD, sig], [pst])
        kb.act(e3[:], psc[:, :], AF.Exp, [psc], [e3], scale=CDEC)
        kb.act(e4[:], psc[:, :], AF.Exp, [psc], [e4], scale=-CDEC)
        kb.act(e5[:], pst[:, :], AF.Exp, [pst], [e5], scale=CDEC)
        kb.act(e6[:], sig[:], AF.Exp, [sig], [e6], scale=-CDEC)
        yield
        for c in range(2):
            rows = slice(64 * c, 64 * c + 64)
            ps = kb.psn("R")
            for hf in range(2):
                for h in range(8):
                    kb.mm(ps[64 * hf:64 * hf + 64, 2 * h:2 * h + 2], sig[rows, h * 64:(h + 1) * 64], ONES[rows, 0:2],
                          True, True, [sig, ONES], [ps])
            kb.act(gC[c][:].unsqueeze(2), v3(ps[:, 0:16], 8)[:, :, 0:1], AF.Exp, [ps], [gC[c]], scale=CDEC)
        kb.tt("dve", e6[:], e6[:], e3[:], ALU.mult, [e6, e3], [e6])
        kb.stt(kkn[:], kkn[:], -1.0, e6[:], ALU.mult, ALU.mult, [kkn, e6], [kkn])
        A_tok = kkn
        kb.tt("pool", e3[:], e3[:], r_, ALU.mult, [e3, yl], [e3])
        R_tok = e3
        kb.tt("dve", e5[:], e5[:], e4[:], ALU.mult, [e5, e4], [e5])
        kb.tt("pool", e6[:], beta[:], e4[:], ALU.mult, [beta, e4], [e6])
        B_tok = e6
        kb.tt("dve", Bt[:], beta[:], e5[:], ALU.mult, [beta, e5], [Bt])
        kb.tt("dve", Kt[:], kp[:], e5[:], ALU.mult, [kp, e5], [Kt])
        kb.tt("pool", e4[:], e4[:], kp[:], ALU.mult, [e4, kp], [e4])
        K_tok = e4
        kb.cp("act", Vb[:], v_, [yl], [Vb])
        yield
        for q, src in (("A", A_tok), ("R", R_tok), ("B", B_tok), ("K", K_tok)):
            for c in range(2):
                rows = slice(64 * c, 64 * c + 64)
                ps = kb.psn("R")
                for h in range(8):
                    kb.tr(ps[0:64, h * 64:(h + 1) * 64], src[rows, h * 64:(h + 1) * 64], ident[rows, rows], [src, ident], [ps])
                kb.cp("act" if c == 0 else "dve", FM[q][c][0:64], v3(ps[0:64, :], 8), [ps], [FM[q][c]])
                if c == 1 and q in "AR":
                    kb.dma("sp", FM[q][c][64:128], FM[q][c][0:64], r=[FM[q][c]], w=[FM[q][c]])

        yield
        def gram(lq, rq):
            ps = kb.psn("R")
            for c in range(2):
                rows = slice(64 * c, 64 * c + 64)
                for h in range(8):
                    kb.mm(ps[rows, h * 64:(h + 1) * 64], FM[lq][c][0:64, h, :], FM[rq][c][0:64, h, :], True, True,
                          [FM[lq][c], FM[rq][c]], [ps])
            return ps
        ps = gram("B", "A")
        kb.tt("dve", Y[0][:], v3(ps[:, :], 8), SU[:], ALU.mult, [ps, SU], [Y[0]])
        ps = gram("A", "B")
        kb.tt("dve", YT[0][:], v3(ps[:, :], 8), SL[:], ALU.mult, [ps, SL], [YT[0]])
        ps = gram("K", "A")
        kb.tt("dve", Mkb[:], v3(ps[:, :], 8), SU[:], ALU.mult, [ps, SU], [Mkb])
        ps = gram("B", "R")
        kb.tt("dve", Nbr[:], v3(ps[:, :], 8), IU[:], ALU.mult, [ps, IU], [Nbr])
        ps = gram("K", "R")
        kb.tt("dve", Nkr[:], v3(ps[:, :], 8), IU[:], ALU.mult, [ps, IU], [Nkr])
        kb.tt("pool", Xi[:], Y[0][:], EYE[:], ALU.add, [Y[0], EYE], [Xi])

        yield
        def mm3(la, ra):
            ps = kb.psn("R")
            for c in range(2):
                rows = slice(64 * c, 64 * c + 64)
                for h in range(8):
                    kb.mm(ps[rows, h * 64:(h + 1) * 64], la[rows, h, :], ra[rows, h, :], True, True, [la, ra], [ps])
            return ps
        for lv in range(1, 6):
            po, pn = (lv - 1) % 2, lv % 2
            psy = mm3(YT[po], Y[po]) if lv < 5 else None
            pyt = mm3(Y[po], YT[po])
            psx = mm3(YT[po], Xi) if lv >= 2 else None
            if psy is not None:
                kb.cp("act", Y[pn][:], v3(psy[:, :], 8), [psy], [Y[pn]])
            kb.cp("act", YT[pn][:], v3(pyt[:, :], 8), [pyt], [YT[pn]])
            if psx is not None:
                kb.tt("dve", Xi[:], Xi[:], v3(psx[:, :], 8), ALU.add, [Xi, psx], [Xi])
            yield
        psx = mm3(YT[1], Xi)
        kb.tt("dve", Xb[:], Xi[:], v3(psx[:, :], 8), ALU.add, [Xi, psx], [Xb])
        yield
        kb.tt("pool", e5[:], r_, kp[:], ALU.mult, [yl, kp], [e5])
        kb.tt("pool", e5[:], e5[:], rkw[:], ALU.mult, [e5, rkw], [e5])
        kb.red("dve", sm[3][:], v3(e5[:], 8), ALU.add, [e5], [sm[3]])
        kb.tt("dve", v3(BG[:], 8), v3(v_, 8), bc_last(sm[3][:], 64), ALU.mult, [yl, sm[3]], [BG])
        kb.tt("pool", BG[:], BG[:], lnb[:], ALU.add, [BG, lnb], [BG])
        kb.tt("pool", BG[:], BG[:], g_t[:], ALU.mult, [BG, g_t], [BG])
        kb.tt("pool", GW[:], g_t[:], lnw[:], ALU.mult, [g_t, lnw], [GW])


    def stageS(i):
        BG, GW, Bt, Kt, Vb, FM, gC, Xb, Mkb, Nbr, Nkr = bind(i)
        pso = kb.psn("O")
        for c in range(2):
            rows = slice(64 * c, 64 * c + 64)
            ps1 = kb.psn("S")
            for h in range(8):
                hs = slice(h * 64, (h + 1) * 64)
                kb.mm(ps1[rows, hs], Mkb[rows, h, :], Vb[rows, hs], True, False, [Mkb, Vb], [ps1])
                kb.mm(ps1[rows, hs], FM["A"][c][rows, h, :], Sb[rows, h, :], False, True, [FM["A"][c], Sb], [ps1])
            kb.cp("act", RH[rows], v3(ps1[rows, :], 8), [ps1], [RH])
            yield
            ps2 = kb.psn("S")
            for h in range(8):
                hs = slice(h * 64, (h + 1) * 64)
                kb.mm(ps2[rows, hs], Xb[rows, h, :], RH[rows, h, :], True, True, [Xb, RH], [ps2])
            kb.cp("dve", Ub[rows], v3(ps2[rows, :], 8), [ps2], [Ub])
            yield
            for h in range(8):
                hs = slice(h * 64, (h + 1) * 64)
                kb.mm(pso[rows, hs], FM["R"][c][rows, h, :], Sb[rows, h, :], True, False, [FM["R"][c], Sb], [pso])
                kb.mm(pso[rows, hs], Nbr[rows, h, :], Ub[rows, h, :], False, False, [Nbr, Ub], [pso])
                kb.mm(pso[rows, hs], Nkr[rows, h, :], Vb[rows, hs], False, True, [Nkr, Vb], [pso])
            ps4 = kb.psn("S")
            for hf in range(2):
                orow = slice(64 * hf, 64 * hf + 64)
                for h in range(8):
                    hs = slice(h * 64, (h + 1) * 64)
                    kb.mm(ps4[orow, hs], Bt[rows, hs], Ub[rows, h, :], True, False, [Bt, Ub], [ps4])
                    kb.mm(ps4[orow, hs], Kt[rows, hs], Vb[rows, hs], False, True, [Kt, Vb], [ps4])
            kb.tt("dve", St[:], St[:], bc_last(gC[c][:], 64), ALU.mult, [St, gC[c]], [St])
            kb.tt("dve", St[:], St[:], v3(ps4[:, :], 8), ALU.add, [St, ps4], [St])
            kb.cp("act", Sb[:], St[:], [St], [Sb])
            yield

        yield
        kb.cp("act", o_t[:], pso[:, :], [pso], [o_t])
        o3 = v3(o_t[:], 8)
        kb.red("dve", sm[1][:], o3, ALU.add, [o_t], [sm[1]])
        kb.ts("dve", sm[1][:], sm[1][:], 1.0 / 64, None, ALU.mult, None, [sm[1]], [sm[1]])
        kb.tt("dve", o3, o3, bc_last(sm[1][:], 64), ALU.subtract, [o_t, sm[1]], [o_t])
        kb.act(pe5[:], o_t[:], AF.Square, [o_t], [pe5])
        kb.red("dve", sm[2][:], v3(pe5[:], 8), ALU.add, [pe5], [sm[2]])
        kb.ts("dve", sm[2][:], sm[2][:], 1.0 / 64, LNX_EPS, ALU.mult, ALU.add, [sm[2]], [sm[2]])
        kb.tt("pool", sm[2][:], sm[2][:], NEGH[:], ALU.pow, [sm[2], NEGH], [sm[2]])
        kb.tt("dve", o3, o3, bc_last(sm[2][:], 64), ALU.mult, [o_t, sm[2]], [o_t])
        yield
        kb.tt("pool", o_t[:], o_t[:], GW[:], ALU.mult, [o_t, GW], [o_t])
        kb.tt("pool", o_t[:], o_t[:], BG[:], ALU.add, [o_t, BG], [o_t])
        kb.dma("sp", S["dOR"][i * 128:(i + 1) * 128, :], o_t[:], r=[o_t], w=[S["dOR"]])

    def interleave(gens, weights=None):
        gens = list(gens)
        weights = list(weights) if weights else [1] * len(gens)
        while gens:
            for g_, w_ in list(zip(gens, weights)):
                for _ in range(w_):
                    try:
                        next(g_)
                    except StopIteration:
                        j_ = gens.index(g_)
                        gens.pop(j_)
                        weights.pop(j_)
                        break

    interleave([stageA(0)])
    interleave([stageR(0)] + ([stageA(1)] if dbg_tiles > 1 else []))
    for i in range(dbg_tiles):
        gens = [stageS(i)]
        wts = [PIPE_W[0]]
        if i + 1 < dbg_tiles:
            gens.append(stageR(i + 1))
            wts.append(PIPE_W[1])
        if i + 2 < dbg_tiles:
            gens.append(stageA(i + 2))
            wts.append(PIPE_W[2])
        interleave(gens, wts)
    kb.end()


def declare(nc):
    D = {}

    def din(name, shape, dt=F32):
        D[name] = nc.dram_tensor(name, list(shape), dt, kind="ExternalInput").ap()

    din("x", [T_SEQ, DM])
    din("pos", [128, NT], I32)
    din("invf", [128, 32])
    din("norm1_w", [1, DM])
    din("w_in", [DM, INC])
    din("mu", [1, RCOLS])
    for n in ("w0", "a0", "k_k", "k_a", "r_k", "lnx_w", "lnx_b"):
        din(n, [1, RW])
    din("w_lora_up", [64, RW])
    din("a_lora_up", [64, RW])
    din("g_lora_up", [128, RW])
    din("cmp_pos_k", [32, 64])
    din("cmp_pos_v", [32, 64])
    din("cmp_k_w1", [2048, 256])
    din("cmp_k_w2", [256, 64])
    din("cmp_v_w1", [2048, 256])
    din("cmp_v_w2", [256, 64])
    din("w_out", [DM, DM])
    din("norm2_w", [1, DM])
    din("ffn_w1", [DM, DFF])
    din("ffn_w3", [DM, DFF])
    din("ffn_w2", [DFF, DM])
    din("final_norm_w", [1, DM])
    return D


def build(stage=3, dbg_tiles=NT):
    nc = bass.Bass("TRN2", target_bir_lowering=False)
    D = declare(nc)
    kb = KB(nc)
    dbg = stage < 3
    kind = "ExternalOutput" if dbg else "Internal"
    S = {
        "dQT": kb.dram("dQT", [128, 4, T_SEQ], BF16, kind),
        "dKT": kb.dram("dKT", [128, 4, T_SEQ], BF16, kind),
        "dV": kb.dram("dV", [128, NT, 2, 2, 65], BF16, kind),
        "dG": kb.dram("dG", [128, NT, 24], F32, kind),
        "dOR": kb.dram("dOR", [T_SEQ, RW], F32, kind),
    }
    S["dKC"] = kb.dram("dKC", [128, 256], BF16, kind)
    S["dVC"] = kb.dram("dVC", [128, 2, 2, 129], BF16, kind)
    phase0(kb, D, S, dbg_tiles)
    S["dH1"] = kb.dram("dH1", [T_SEQ, DM], F32, kind)
    if stage >= 2:
        phase1a(kb, D, S)
    if stage >= 2.5:
        phase1b(kb, D, S, M_LIST)
    if stage >= 3:
        out_ap = nc.dram_tensor("out", [T_SEQ, DM], F32, kind="ExternalOutput").ap()
        phase2(kb, D, S, out_ap)
    kb.gs.close()
    return nc


def host_inputs(inputs):
    g = lambda k: np.ascontiguousarray(np.asarray(inputs[k])[0], dtype=np.float32)
    w_in = g("w_in")
    nsa0 = RCOLS
    qperm = []
    for p in range(4):
        qperm += list(range(nsa0 + p * 64, nsa0 + p * 64 + 64))
        qperm += list(range(nsa0 + (p + 4) * 64, nsa0 + (p + 4) * 64 + 64))
    o = nsa0 + 512
    kc, vc, ks, vs, kw, vw = (list(range(o + j * 128, o + (j + 1) * 128)) for j in range(6))
    gl = list(range(o + 768, o + 792))
    perm = list(range(RCOLS)) + qperm + kc + ks + kw + vc + vs + vw + gl
    assert len(perm) == INC
    w_in_p = np.ascontiguousarray(w_in[:, perm])
    invf = (10000.0 ** (-np.arange(32, dtype=np.float32) / 32)).astype(np.float32)
    common = {
        "invf": np.ascontiguousarray(np.broadcast_to(invf[None, :], (128, 32))),
        "norm1_w": g("norm1_w")[None, :],
        "w_in": w_in_p,
        "mu": g("mu_rwkv")[None, :],
        "w0": g("w0")[None, :], "a0": g("a0")[None, :], "k_k": g("k_k")[None, :], "k_a": g("k_a")[None, :],
        "r_k": g("r_k").reshape(1, RW), "lnx_w": g("lnx_w")[None, :], "lnx_b": g("lnx_b")[None, :],
        "w_lora_up": g("w_lora_up"), "a_lora_up": g("a_lora_up"), "g_lora_up": g("g_lora_up"),
        "cmp_pos_k": g("cmp_pos_k"), "cmp_pos_v": g("cmp_pos_v"),
        "cmp_k_w1": g("cmp_k_w1"), "cmp_k_w2": g("cmp_k_w2"), "cmp_v_w1": g("cmp_v_w1"), "cmp_v_w2": g("cmp_v_w2"),
        "w_out": g("w_out"), "norm2_w": g("norm2_w")[None, :],
        "ffn_w1": g("ffn_w1"), "ffn_w3": g("ffn_w3"), "ffn_w2": g("ffn_w2"),
        "final_norm_w": np.asarray(inputs["final_norm_w"], dtype=np.float32).reshape(1, DM),
    }
    x = np.asarray(inputs["x"], dtype=np.float32)
    pos = np.asarray(inputs["positions"]).astype(np.int32)
    maps = []
    for b in range(x.shape[0]):
        m = dict(common)
        m["x"] = np.ascontiguousarray(x[b])
        m["pos"] = np.ascontiguousarray(pos[b].reshape(NT, 128).T)
        maps.append(m)
    return maps


def kernel(**inputs):
    nc = build(3)
    maps = host_inputs(inputs)
    res = run_bass_kernel_spmd(nc, maps, core_ids=list(range(len(maps))))
    return np.stack([np.asarray(r["out"]) for r in res.results], axis=0).astype(np.float32)


def phase1a(kb, D, S):
    kb.begin()
    ident = kb.sb([128, 128], name="ident")
    kb.memset("pool", ident[:], 0.0, [ident])
    kb.asel(ident[:], ident[:], [[-1, 128]], ALU.not_equal, 1.0, 0, 1, [ident], [ident])
    KT = kb.sb([128, 2, T_SEQ], BF16, nslots=2, name="KTc")
    kb.dma("sp", KT[:, 0, :], S["dKT"][:, 0, :], r=[S["dKT"]], w=[KT.b[0]])
    kb.dma("sp", KT[:, 1, :], S["dKT"][:, 3, :], r=[S["dKT"]], w=[KT.b[1]])
    W1 = {}
    W2 = {}
    PT = {}
    for z, nm in (("k", "cmp_k"), ("v", "cmp_v")):
        w1 = kb.sb([128, 32, 256], BF16, name="w1" + z)
        src = D[nm + "_w1"].rearrange("(l d) h -> d l h", d=64)
        for hf in range(2):
            for lq in range(4):
                kb.dma("pool", w1[64 * hf:64 * hf + 64, lq * 8:(lq + 1) * 8, :], src[:, lq * 8:(lq + 1) * 8, :], w=[w1])
        W1[z] = w1
        w2 = kb.sb([128, 2, 64], BF16, name="w2" + z)
        kb.dma("pool", w2[:], D[nm + "_w2"].rearrange("(c p) d -> p c d", p=128), w=[w2])
        W2[z] = w2
        pos = kb.sb([32, 64], name="pos" + z)
        kb.dma("sp", pos[:], D["cmp_pos_" + z][:, :], w=[pos])
        ps = kb.psn()
        kb.tr(ps[0:64, 0:32], pos[:, :], ident[0:32, 0:32], [pos, ident], [ps])
        pT = kb.sb([64, 32, 2], BF16, name="posT" + z)
        for j in range(2):
            kb.cp("dve", pT[:, :, j:j + 1], ps[0:64, 0:32].unsqueeze(2), [ps], [pT])
        PT[z] = pT
    OV = kb.sb([128, 2, 64], name="OV")
    kb.memset("pool", OV[:], 1.0, [OV])
    for nt in range(2):
        kb.asel(OV[:, nt, :], OV[:, nt, :], [[-4, 64]], ALU.is_ge, 0.0, 128 * nt + 1, 1, [OV], [OV])
        kb.asel(OV[:, nt, :], OV[:, nt, :], [[4, 64]], ALU.is_ge, 0.0, 3 - 128 * nt, -1, [OV], [OV])
    VC = kb.sb([128, 2, 2, 129], BF16, name="VC")
    kb.memset("pool", VC[:], 1.0, [VC])
    for nt in range(2):
        for g in range(2):
            kb.cp("pool", VC[:, nt, g, 65:129], OV[:, nt, :], [OV], [VC])
    KCT = kb.sb([128, 256], BF16, name="KCT")
    kb.memset("pool", KCT[:], 0.0, [KCT])
    bias = kb.sb([128, 1], name="bias")
    hb = kb.sb([128, 255], name="hb")
    t1 = kb.sb([128, 255], name="gt1")
    t2 = kb.sb([128, 255], name="gt2")
    GH = kb.sb([128, 2, 256], BF16, name="GH")
    for zi, z in enumerate(("k", "v")):
        for g in range(2):
            rows = slice(64 * g, 64 * g + 64)
            for hc in range(2):
                hcs = slice(hc * 128, (hc + 1) * 128)
                psb = kb.psn()
                for l in range(32):
                    kb.mm(psb[:, 0:2], W1[z][0:64, l, hcs], PT[z][:, l, :], l == 0, l == 31, [W1[z], PT[z]], [psb])
                kb.cp("dve", bias[:], psb[:, 0:1], [psb], [bias])
                ps = kb.psn()
                for l in range(32):
                    kb.mm(ps[:, 0:255], W1[z][rows, l, hcs], KT[rows, zi, l:l + 4065:16], l == 0, l == 31,
                          [W1[z], KT.b[zi]], [ps])
                kb.act(hb[:], ps[:, 0:255], AF.Identity, [ps, bias], [hb], bias=bias[:, 0:1])
                kb.tt("dve", t1[:], hb[:], hb[:], ALU.mult, [hb], [t1])
                kb.ts("dve", t1[:], t1[:], 0.044715, 1.0, ALU.mult, ALU.add, [t1], [t1])
                kb.tt("dve", t1[:], t1[:], hb[:], ALU.mult, [t1, hb], [t1])
                kb.act(t2[:], t1[:], AF.Tanh, [t1], [t2], scale=0.7978845608028654)
                kb.stt(t1[:], t2[:], 1.0, hb[:], ALU.add, ALU.mult, [t2, hb], [t1])
                kb.ts("dve", GH[:, hc, 0:255], t1[:], 0.5, None, ALU.mult, None, [t1], [GH])
            if z == "k":
                ps = kb.psn()
                for hc in range(2):
                    kb.mm(ps[rows, 0:255], W2[z][:, hc, :], GH[:, hc, 0:255], hc == 0, hc == 1, [W2[z], GH], [ps])
                kb.cp("act", KCT[rows, 0:255], ps[rows, 0:255], [ps], [KCT])
            else:
                for nt in range(2):
                    kn = 128 if nt == 0 else 127
                    ps = kb.psn()
                    for hc in range(2):
                        kb.mm(ps[0:kn, 0:64], GH[:, hc, nt * 128:nt * 128 + kn], W2[z][:, hc, :], hc == 0, hc == 1,
                              [GH, W2[z]], [ps])
                    kb.cp("act", VC[0:kn, nt, g, 0:64], ps[0:kn, 0:64], [ps], [VC])
    kb.dma("sp", S["dKC"][:, :], KCT[:], r=[KCT], w=[S["dKC"]])
    kb.dma("sp", S["dVC"][:, :, :, :], VC[:], r=[VC], w=[S["dVC"]])
    kb.end()


def phase1b(kb, D, S, m_list=None):
    kb.begin()
    ident = kb.sb([128, 128], name="ident")
    kb.memset("pool", ident[:], 0.0, [ident])
    kb.asel(ident[:], ident[:], [[-1, 128]], ALU.not_equal, 1.0, 0, 1, [ident], [ident])
    qT = kb.sb([128, 4, T_SEQ], BF16, name="qT")
    for p in range(4):
        kb.dma("sp", qT[:, p, :], S["dQT"][:, p, :], r=[S["dQT"]], w=[qT])
    KT = kb.sb([128, 2, 2, T_SEQ], BF16, name="KT")
    kb.memset("pool", KT[64:128, 1, 0, :], 0.0, [KT])
    kb.memset("pool", KT[0:64, 1, 1, :], 0.0, [KT])
    for z, slot in ((0, 1), (1, 2)):
        kb.dma("sp", KT[0:64, z, 0, :], S["dKT"][0:64, slot, :], r=[S["dKT"]], w=[KT])
        kb.dma("sp", KT[64:128, z, 1, :], S["dKT"][64:128, slot, :], r=[S["dKT"]], w=[KT])
    VA = kb.sb([128, NT, 2, 2, 65], BF16, name="VA")
    for q4 in range(4):
        kb.dma("sp", VA[:, q4 * 8:(q4 + 1) * 8], S["dV"][:, q4 * 8:(q4 + 1) * 8], r=[S["dV"]], w=[VA])
    G = kb.sb([128, NT, 24], name="G")
    kb.dma("sp", G[:], S["dG"][:, :, :], r=[S["dG"]], w=[G])
    KCT = kb.sb([128, 256], BF16, name="KCT")
    kb.dma("sp", KCT[:], S["dKC"][:, :], r=[S["dKC"]], w=[KCT])
    VC = kb.sb([128, 2, 2, 129], BF16, name="VC")
    kb.dma("sp", VC[:], S["dVC"][:, :, :, :], r=[S["dVC"]], w=[VC])
    WOUT = kb.sb([128, 8, DM], BF16, nslots=8, name="WOUT")
    for k in range(8):
        kb.dma("pool", WOUT[:, k, :], D["w_out"][k * 128:(k + 1) * 128, :], w=[WOUT.b[k]])
    Eexp = kb.sb([128, T_SEQ], BF16, name="Eexp")
    kb.memset("pool", Eexp[0:64], 1.0, [Eexp])
    kb.asel(Eexp[0:64], Eexp[0:64], [[1, T_SEQ]], ALU.is_ge, 0.0, 0, -64, [Eexp], [Eexp])
    kb.asel(Eexp[0:64], Eexp[0:64], [[-1, T_SEQ]], ALU.is_ge, 0.0, 63, 64, [Eexp], [Eexp])
    kb.dma("sp", Eexp[64:128], Eexp[0:64], r=[Eexp], w=[Eexp])
    kb.dma("sp", KT[64:128, 0, 0, :], Eexp[64:128, :], r=[Eexp], w=[KT])
    kb.dma("sp", KT[0:64, 0, 1, :], Eexp[0:64, :], r=[Eexp], w=[KT])
    Wb = kb.sb([128, 126], name="Wb")
    for hi in range(2):
        rows = slice(64 * hi, 64 * hi + 64)
        kb.memset("pool", Wb[rows, 0:61 + hi], 0.0, [Wb])
        kb.memset("pool", Wb[rows, 61 + hi:63 + hi], 1e4, [Wb])
        kb.memset("pool", Wb[rows, 63 + hi:126], -1e30, [Wb])

    Pc = [kb.sb([128, 4, 128], BF16, name=f"Pc{j}") for j in range(2)]
    PTs2 = [kb.sb([128, NT, 512], BF16, nslots=NT, name=f"PTs{j}") for j in range(2)]
    PTw2 = [kb.sb([128, 5, 512], BF16, nslots=5, name=f"PTw{j}") for j in range(2)]
    rz = [kb.sb([128, 4], name=f"rz{j}") for j in range(2)]
    cf = [kb.sb([128, 4], name=f"cf{j}") for j in range(2)]
    t3 = [kb.sb([128, 4, 64], name=f"t3{j}") for j in range(2)]
    imp = kb.sb([128, 64], name="imp")
    imp2 = kb.sb([128, 64], name="imp2")
    m8 = kb.sb([128, 8], name="m8")
    m8b = kb.sb([128, 8], name="m8b")
    sel = kb.sb([128, 2, 64], name="sel")
    selT2 = [kb.sb([128, 4, 128], BF16, name=f"selT{j}") for j in range(2)]
    ON2 = [kb.sb([128, 512], name=f"ON{j}") for j in range(2)]
    ORt = [kb.sb([128, 512], name="ORt0")] * 2
    XT = [kb.sb([128, DM], name="xt0")] * 2
    oT = kb.sb([128, 8, 128], BF16, name="oT")
    h1 = kb.sb([128, DM], name="h1")
    kb.set_pools({"C": [0, 1], "S": [2, 3, 4, 5], "V": [6, 7]})

    def finish_branch(psO, g, m, br, first, st):
        ON = ON2[m % 2]
        O3 = v3(psO[:, 0:260], 4)
        kb.ts("dve", rz[st][:].unsqueeze(2), O3[:, :, 64:65], 1e-30, None, ALU.add, None, [psO], [rz[st]])
        kb.op("dve", lambda e: e.reciprocal(rz[st][:], rz[st][:]), [rz[st]], [rz[st]])
        gate3 = v3(G[:, m, :], 8)
        kb.tt("dve", cf[st][:].unsqueeze(2), gate3[:, 4 * g:4 * g + 4, br:br + 1], rz[st][:].unsqueeze(2), ALU.mult,
              [G, rz[st]], [cf[st]])
        ONg = v3(ON[:, g * 256:(g + 1) * 256], 4)
        if first:
            kb.tt("dve", ONg, O3[:, :, 0:64], bc_last(cf[st][:], 64), ALU.mult, [psO, cf[st]], [ON])
        else:
            kb.tt("dve", t3[st][:], O3[:, :, 0:64], bc_last(cf[st][:], 64), ALU.mult, [psO, cf[st]], [t3[st]])
            kb.tt("pool", ONg, ONg, t3[st][:], ALU.add, [ON, t3[st]], [ON])

    def stageC(m, g, it):
        qs = slice(m * 128, (m + 1) * 128)
        grow = slice(64 * g, 64 * g + 64)
        selT = selT2[it % 2]
        nts = [0] + ([1] if m >= 16 else [])
        for nt in nts:
            kn = 128 if nt == 0 else 127
            ps = kb.psn("C")
            kb.mm(ps[0:kn, :], KCT[grow, nt * 128:nt * 128 + kn], qT[grow, :, qs], True, True, [KCT, qT], [ps])
            kb.act(Pc[nt][0:kn], v3(ps[0:kn, :], 4), AF.Exp, [ps], [Pc[nt]], scale=0.125)
            if not (nt == 0 and m >= 17):
                kb.asel(Pc[nt][0:kn], Pc[nt][0:kn], [[0, 4], [1, 128]], ALU.is_ge, 0.0,
                        128 * m - 2048 * nt - 31, -16, [Pc[nt]], [Pc[nt]])
        yield
        psA = kb.psn("C")
        psB = kb.psn("C")
        for h in range(4):
            for ii, nt in enumerate(nts):
                kn = 128 if nt == 0 else 127
                kb.mm(psA[:, h * 65:(h + 1) * 65], Pc[nt][0:kn, h, :], VC[0:kn, nt, g, 0:65], ii == 0,
                      ii == len(nts) - 1, [Pc[nt], VC], [psA])
        for h in range(4):
            for ii, nt in enumerate(nts):
                kn = 128 if nt == 0 else 127
                kb.mm(psB[:, h * 64:(h + 1) * 64], Pc[nt][0:kn, h, :], VC[0:kn, nt, g, 65:129], ii == 0,
                      ii == len(nts) - 1, [Pc[nt], VC], [psB])
        finish_branch(psA, g, m, 0, True, 0)
        yield
        kb.tt("dve", t3[0][:], v3(psB[:, 0:256], 4), bc_last(rz[0][:], 64), ALU.mult, [psB, rz[0]], [t3[0]])
        kb.red("dve", imp[:], t3[0][:].rearrange("p h j -> p j h"), ALU.add, [t3[0]], [imp])
        kb.tt("dve", imp[:], imp[:], Wb[:, 62 - 2 * m:126 - 2 * m], ALU.add, [imp, Wb], [imp])
        if m >= 1:
            kb.ts("dve", imp[:, 0:1], imp[:, 0:1], 1e4, None, ALU.add, None, [imp], [imp])
        yield
        kb.op("dve", lambda e: e.max(m8[:], imp[:]), [imp], [m8])
        kb.op("dve", lambda e: e.match_replace(imp2[:], m8[:], imp[:], -1e30), [imp, m8], [imp2])
        kb.op("dve", lambda e: e.max(m8b[:], imp2[:]), [imp2], [m8b])
        kb.ts("dve", sel[:], imp[:].unsqueeze(1).to_broadcast([128, 2, 64]), m8b[:, 7:8], -800.0, ALU.is_lt, ALU.mult,
              [imp, m8b], [sel])
        yield
        ps = kb.psn("C")
        kb.tr(ps[:, 0:128], sel[:].rearrange("p a j -> p (a j)"), ident[:], [sel, ident], [ps])
        orow = slice(64, 128) if g == 0 else slice(0, 64)
        kb.cp("act", selT[orow], ps[orow, 0:128].unsqueeze(1).to_broadcast([64, 4, 128]), [ps], [selT])
        kb.cp("dve", selT[grow], qT[grow, :, qs], [qT], [selT])

    def stageS(m, g, it):
        qs = slice(m * 128, (m + 1) * 128)
        selT = selT2[it % 2]
        PTs, PTw = PTs2[it % 2], PTw2[it % 2]
        if g == 1:
            kb.dma("sp", ORt[0][:], S["dOR"][qs, :], r=[S["dOR"]], w=[ORt[0]])
            kb.dma("sp", XT[0][:], D["x"][qs, :], w=[XT[0]])
        kt0 = max(0, m - 4)
        for kt in range(kt0, m + 1):
            ksl = slice(kt * 128, (kt + 1) * 128)
            sl = kt - kt0
            psS = kb.psn("S")
            kb.mm(psS[:, :], KT[:, 1, g, ksl], qT[:, :, qs], True, True, [KT, qT], [psS])
            kb.act(PTw[:, sl, :], psS[:, :], AF.Exp, [psS], [PTw.b[sl]], scale=0.125)
            if kt == m:
                kb.asel(v3(PTw[:, sl, :], 4), v3(PTw[:, sl, :], 4), [[0, 4], [1, 128]], ALU.is_ge, 0.0, 0, -1,
                        [PTw.b[sl]], [PTw.b[sl]])
            if kt == m - 4:
                kb.asel(v3(PTw[:, sl, :], 4), v3(PTw[:, sl, :], 4), [[0, 4], [-1, 128]], ALU.is_ge, 0.0, -1, 1,
                        [PTw.b[sl]], [PTw.b[sl]])
            yield
        for kt in range(m + 1):
            ksl = slice(kt * 128, (kt + 1) * 128)
            psS = kb.psn("S")
            kb.mm(psS[:, :], KT[:, 0, g, ksl], selT[:, :, :], True, True, [KT, selT], [psS])
            kb.act(PTs[:, kt, :], psS[:, :], AF.Exp, [psS], [PTs.b[kt]], scale=0.125)
            if kt == m:
                kb.asel(v3(PTs[:, kt, :], 4), v3(PTs[:, kt, :], 4), [[0, 4], [1, 128]], ALU.is_ge, 0.0, 0, -1,
                        [PTs.b[kt]], [PTs.b[kt]])
            yield

    def stageV(m, g, it):
        qs = slice(m * 128, (m + 1) * 128)
        PTs, PTw = PTs2[it % 2], PTw2[it % 2]
        ON = ON2[m % 2]
        cur = m % 2
        kt0 = max(0, m - 4)
        psO = kb.psn("V")
        for h in range(4):
            for kt in range(kt0, m + 1):
                sl = kt - kt0
                kb.mm(psO[:, h * 65:(h + 1) * 65], PTw[:, sl, h * 128:(h + 1) * 128], VA[:, kt, 1, g, :],
                      kt == kt0, kt == m, [PTw.b[sl], VA], [psO])
        finish_branch(psO, g, m, 2, False, 1)
        yield
        psO = kb.psn("V")
        for h in range(4):
            for kt in range(m + 1):
                kb.mm(psO[:, h * 65:(h + 1) * 65], PTs[:, kt, h * 128:(h + 1) * 128], VA[:, kt, 0, g, :],
                      kt == 0, kt == m, [PTs.b[kt], VA], [psO])
            yield
        finish_branch(psO, g, m, 1, False, 1)
        yield
        if g == 0:
            return
        for half in range(2):
            ps = kb.psn("V")
            for kk_ in range(4):
                src = ORt[cur] if half == 0 else ON
                kb.tr(ps[:, kk_ * 128:(kk_ + 1) * 128], src[:, kk_ * 128:(kk_ + 1) * 128], ident[:], [src, ident], [ps])
            kb.cp("act", oT[:, half * 4:(half + 1) * 4, :], v3(ps[:, :], 4), [ps], [oT])
        yield
        for half in range(2):
            ps = kb.psn("V")
            for k in range(8):
                kb.mm(ps[:, :], oT[:, k, :], WOUT[:, k, half * 512:(half + 1) * 512], k == 0, k == 7, [oT, WOUT.b[k]], [ps])
            kb.tt("dve", h1[:, half * 512:(half + 1) * 512], XT[cur][:, half * 512:(half + 1) * 512], ps[:, :], ALU.add,
                  [XT[cur], ps], [h1])
        kb.dma("pool", S["dH1"][qs, :], h1[:], r=[h1], w=[S["dH1"]])

    def interleave(gens):
        gens = list(gens)
        while gens:
            for g_ in list(gens):
                try:
                    next(g_)
                except StopIteration:
                    gens.remove(g_)

    its = [(m, g) for m in (m_list if m_list is not None else range(NT)) for g in range(2)]
    N_ = len(its)
    interleave([stageC(its[0][0], its[0][1], 0)])
    interleave([stageS(its[0][0], its[0][1], 0)] + ([stageC(its[1][0], its[1][1], 1)] if N_ > 1 else []))
    for n_ in range(N_):
        gens = [stageV(its[n_][0], its[n_][1], n_)]
        if n_ + 1 < N_:
            gens.append(stageS(its[n_ + 1][0], its[n_ + 1][1], n_ + 1))
        if n_ + 2 < N_:
            gens.append(stageC(its[n_ + 2][0], its[n_ + 2][1], n_ + 2))
        interleave(gens)
    kb.end()


def phase2(kb, D, S, out_ap, n_super=8):
    kb.begin()
    ident = kb.sb([128, 128], name="ident")
    kb.memset("pool", ident[:], 0.0, [ident])
    kb.asel(ident[:], ident[:], [[-1, 128]], ALU.not_equal, 1.0, 0, 1, [ident], [ident])
    NF = DFF // 128
    W1 = kb.sb([128, 8, DFF], BF16, nslots=8, name="W1")
    W3 = kb.sb([128, 8, DFF], BF16, nslots=8, name="W3")
    W2 = kb.sb([128, NF, DM], BF16, nslots=NF, name="W2")
    for k in range(8):
        kb.dma("pool", W1[:, k, :], D["ffn_w1"][k * 128:(k + 1) * 128, :], w=[W1.b[k]])
        kb.dma("pool", W3[:, k, :], D["ffn_w3"][k * 128:(k + 1) * 128, :], w=[W3.b[k]])
    for f in range(NF):
        kb.dma("pool", W2[:, f, :], D["ffn_w2"][f * 128:(f + 1) * 128, :], w=[W2.b[f]])
    n2w = kb.sb([128, DM], name="n2w")
    kb.dma("sp", n2w[:], D["norm2_w"][0:1, :].partition_broadcast(128), w=[n2w])
    fnw = kb.sb([128, DM], name="fnw")
    kb.dma("sp", fnw[:], D["final_norm_w"][0:1, :].partition_broadcast(128), w=[fnw])
    H = [kb.sb([128, 4, DM], nslots=4, name="H0")]
    un = kb.sb([128, DM], name="un")
    ssq = kb.sb([128, 1], name="ssq2")
    rstd = kb.sb([128, 1], name="rstd2")
    uT = kb.sb([128, 8, 512], BF16, name="uT")
    hT = kb.sb([128, NF, 512], BF16, nslots=NF, name="hT")
    sg = [kb.sb([128, 512], name=f"sg{j}") for j in range(2)]
    h2 = [kb.sb([128, DM], name="h20")]
    ot = [kb.sb([128, DM], name="ot0")]

    for s_ in range(n_super):
        Hs = H[0]
        for j in range(4):
            r0 = s_ * 512 + j * 128
            kb.dma("sp", Hs[:, j, :], S["dH1"][r0:r0 + 128, :], r=[S["dH1"]], w=[Hs.b[j]])
        for j in range(4):
            kb.act(un[:], Hs[:, j, :], AF.Square, [Hs.b[j]], [un, ssq], accum_out=ssq[:])
            kb.ts("dve", rstd[:], ssq[:], 1.0 / DM, RMS_EPS, ALU.mult, ALU.add, [ssq], [rstd])
            kb.act(rstd[:], rstd[:], AF.Sqrt, [rstd], [rstd])
            kb.op("dve", lambda e: e.reciprocal(rstd[:], rstd[:]), [rstd], [rstd])
            kb.stt(un[:], Hs[:, j, :], rstd[:, 0:1], n2w[:], ALU.mult, ALU.mult, [Hs.b[j], rstd, n2w], [un])
            for half in range(2):
                ps = kb.psn()
                for kk_ in range(4):
                    k = half * 4 + kk_
                    kb.tr(ps[:, kk_ * 128:(kk_ + 1) * 128], un[:, k * 128:(k + 1) * 128], ident[:], [un, ident], [ps])
                kb.cp("act" if half == 0 else "dve", uT[:, half * 4:(half + 1) * 4, j * 128:(j + 1) * 128],
                      v3(ps[:, :], 4), [ps], [uT])
        for f in range(NF):
            fs = slice(f * 128, (f + 1) * 128)
            pa = kb.psn()
            for k in range(8):
                kb.mm(pa[:, :], W1[:, k, fs], uT[:, k, :], k == 0, k == 7, [W1.b[k], uT], [pa])
            pb = kb.psn()
            for k in range(8):
                kb.mm(pb[:, :], W3[:, k, fs], uT[:, k, :], k == 0, k == 7, [W3.b[k], uT], [pb])
            kb.act(sg[f % 2][:], pa[:, :], AF.Silu, [pa], [sg[f % 2]])
            kb.tt("dve", hT[:, f, :], sg[f % 2][:], pb[:, :], ALU.mult, [sg[f % 2], pb], [hT.b[f]])
        for j in range(4):
            r0 = s_ * 512 + j * 128
            hh = h2[0]
            for half in range(2):
                ps = kb.psn()
                for f in range(NF):
                    kb.mm(ps[:, :], hT[:, f, j * 128:(j + 1) * 128], W2[:, f, half * 512:(half + 1) * 512], f == 0,
                          f == NF - 1, [hT.b[f], W2.b[f]], [ps])
                kb.tt("dve", hh[:, half * 512:(half + 1) * 512], Hs[:, j, half * 512:(half + 1) * 512], ps[:, :], ALU.add,
                      [Hs.b[j], ps], [hh])
            oo = ot[0]
            kb.act(oo[:], hh[:], AF.Square, [hh], [oo, ssq], accum_out=ssq[:])
            kb.ts("dve", rstd[:], ssq[:], 1.0 / DM, RMS_EPS, ALU.mult, ALU.add, [ssq], [rstd])
            kb.act(rstd[:], rstd[:], AF.Sqrt, [rstd], [rstd])
            kb.op("dve", lambda e: e.reciprocal(rstd[:], rstd[:]), [rstd], [rstd])
            kb.stt(oo[:], hh[:], rstd[:, 0:1], fnw[:], ALU.mult, ALU.mult, [hh, rstd, fnw], [oo])
            kb.dma("sp", out_ap[r0:r0 + 128, :], oo[:], r=[oo])
    kb.end()
```

```python
from contextlib import ExitStack
import math
import numpy as np
import concourse.bass as bass
import concourse.mybir as mybir
from concourse.bass_utils import run_bass_kernel_spmd

F32 = mybir.dt.float32
BF16 = mybir.dt.bfloat16
I32 = mybir.dt.int32
AF = mybir.ActivationFunctionType
ALU = mybir.AluOpType
AX = mybir.AxisListType

NDS = 48
T_SEQ = 4096
NT = 32
DM = 1024
RW = 512
RCOLS = 1792
INC = 3096
DFF = 2816
CDEC = -math.exp(-0.5)
RMS_EPS = 1e-6
LNX_EPS = 64e-5


class Buf:
    __slots__ = ("w", "r", "name")

    def __init__(self, name=""):
        self.w = None
        self.r = []
        self.name = name


class T:
    def __init__(self, h, nslots=1, name=""):
        self.h = h
        self.b = [Buf(f"{name}.{i}") for i in range(nslots)]

    def __getitem__(self, k):
        return self.h[k]


class KB:
    ENG = ("pe", "dve", "act", "pool", "sp")

    def __init__(self, nc):
        self.nc = nc
        self.gs = ExitStack()
        self.es = None
        self.ops = {e: [] for e in self.ENG}
        self.cnt = {e: 0 for e in ("pe", "dve", "act", "pool")}
        self.seen = {e: {} for e in self.ENG}
        self.sem = {}
        for e in ("pe", "dve", "act", "pool"):
            self.sem[e] = self.gs.enter_context(nc.semaphore("s_" + e))
        self.dsem = [self.gs.enter_context(nc.semaphore(f"d{i}")) for i in range(NDS)]
        self.dcnt = [0] * NDS
        self.dq = {"sp": list(range(0, NDS // 2)), "pool": list(range(NDS // 2, NDS))}
        self.dnext = {"sp": 0, "pool": 0}
        self.same_engine_sync = True
        self.same_engine_all = True
        self.ntiles = 0
        self.banks = []
        self.bank_i = 0

    def begin(self):
        self.es = ExitStack()
        self.ops = {e: [] for e in self.ENG}
        self.banks = [self.ps() for _ in range(8)]
        self.set_pools({"all": range(8)})

    def end(self):
        self.wait_all("sp", [(("d", k), self.dcnt[k]) for k in range(NDS) if self.dcnt[k]])
        self.emit()
        self.es.close()
        self.es = None

    def set_pools(self, pools):
        self.pools = {k: list(v) for k, v in pools.items()}
        self.pool_i = {k: 0 for k in pools}

    def psn(self, pool="all"):
        ids = self.pools[pool]
        b = self.banks[ids[self.pool_i[pool] % len(ids)]]
        self.pool_i[pool] += 1
        return b

    def sb(self, shape, dtype=F32, nslots=1, name=None):
        self.ntiles += 1
        name = f"sb{self.ntiles}_" + (name or "t")
        h = self.es.enter_context(self.nc.sbuf_tensor(name, list(shape), dtype))
        return T(h, nslots, name)

    def ps(self, shape=(128, 512), dtype=F32, name=None):
        self.ntiles += 1
        name = f"ps{self.ntiles}_" + (name or "p")
        h = self.es.enter_context(self.nc.psum_tensor(name, list(shape), dtype))
        return T(h, 1, name)

    def dram(self, name, shape, dtype, kind="Internal"):
        h = self.nc.dram_tensor(name, list(shape), dtype, kind=kind)
        return T(h.ap(), 1, name)

    def _deps(self, eng, reads, writes):
        deps = set()
        for b in reads:
            if b.w is not None:
                deps.add(b.w)
        for b in writes:
            if b.w is not None and (b.w[0] != eng or self.same_engine_all):
                deps.add(b.w)
            for t_ in b.r:
                if t_[0] != eng or self.same_engine_all:
                    deps.add(t_)
        seen = self.seen[eng]
        best = {}
        for (k, n) in deps:
            if k == eng and (eng == "pe" or not self.same_engine_sync):
                continue
            if seen.get(k, 0) < n:
                best[k] = max(best.get(k, 0), n)
        waits = []
        for k, n in best.items():
            seen[k] = n
            waits.append((k, n))
        return waits

    @staticmethod
    def _mark(tok, reads, writes):
        for b in reads:
            b.r.append(tok)
        for b in writes:
            b.w = tok
            b.r = []

    @staticmethod
    def _bufs(lst):
        out = []
        for x in lst:
            if isinstance(x, T):
                out.extend(x.b)
            elif isinstance(x, Buf):
                out.append(x)
            else:
                out.extend(KB._bufs(x))
        return out

    def op(self, eng, fn, r=(), w=()):
        reads, writes = self._bufs(r), self._bufs(w)
        waits = self._deps(eng, reads, writes)
        self.cnt[eng] += 1
        tok = (eng, self.cnt[eng])
        self._mark(tok, reads, writes)
        self.ops[eng].append((waits, fn, ("c", eng)))
        return tok

    def dma(self, q, out, in_, r=(), w=(), **kw):
        reads, writes = self._bufs(r), self._bufs(w)
        k = self.dq[q][self.dnext[q]]
        self.dnext[q] = (self.dnext[q] + 1) % len(self.dq[q])
        waits = self._deps(q, reads, writes)
        prev = self.dcnt[k]
        if prev and self.seen[q].get(("d", k), 0) < prev:
            self.seen[q][("d", k)] = prev
            waits.append((("d", k), prev))
        self.dcnt[k] += 16
        tok = (("d", k), self.dcnt[k])
        self._mark(tok, reads, writes)
        self.ops[q].append((waits, lambda e: e.dma_start(out=out, in_=in_, **kw), ("d", k)))
        return tok

    def wait_all(self, q, toks):
        waits = []
        for (k, n) in toks:
            if self.seen[q].get(k, 0) < n:
                self.seen[q][k] = n
                waits.append((k, n))
        self.ops[q].append((waits, None, None))

    def _semof(self, k):
        if isinstance(k, tuple):
            return self.dsem[k[1]]
        return self.sem[k]

    def _run(self, e, name):
        for waits, fn, kind in self.ops[name]:
            for (k, n) in waits:
                e.wait_ge(self._semof(k), n)
            if fn is None:
                continue
            ins = fn(e)
            if kind[0] == "c":
                ins.then_inc(self.sem[kind[1]], 1)
            else:
                ins.then_inc(self.dsem[kind[1]], 16)

    def emit(self):
        with self.nc.Block() as block:
            @block.tensor
            def _(e):
                self._run(e, "pe")

            @block.vector
            def _(e):
                self._run(e, "dve")

            @block.scalar
            def _(e):
                self._run(e, "act")

            @block.gpsimd
            def _(e):
                self._run(e, "pool")

            @block.sync
            def _(e):
                self._run(e, "sp")

    def tt(self, eng, out, a, b, op, r, w):
        return self.op(eng, lambda e: e.tensor_tensor(out, a, b, op), r, w)

    def ts(self, eng, out, a, s1, s2, op0, op1, r, w):
        if s2 is None:
            return self.op(eng, lambda e: e.tensor_scalar(out, a, s1, None, op0), r, w)
        return self.op(eng, lambda e: e.tensor_scalar(out, a, s1, s2, op0, op1), r, w)

    def stt(self, out, a, s, b, op0, op1, r, w, eng="dve"):
        return self.op(eng, lambda e: e.scalar_tensor_tensor(out, a, s, b, op0, op1), r, w)

    def act(self, out, in_, func, r, w, **kw):
        return self.op("act", lambda e: e.activation(out, in_, func, **kw), r, w)

    def cp(self, eng, out, in_, r, w):
        if eng == "act":
            return self.op("act", lambda e: e.copy(out, in_), r, w)
        return self.op(eng, lambda e: e.tensor_copy(out, in_), r, w)

    def mm(self, out, lhsT, rhs, start, stop, r, w):
        return self.op("pe", lambda e: e.matmul(out, lhsT, rhs, start=start, stop=stop), r, w)

    def tr(self, out, in_, ident, r, w):
        return self.op("pe", lambda e: e.transpose(out, in_, ident), r, w)

    def red(self, eng, out, in_, op, r, w):
        return self.op(eng, lambda e: e.tensor_reduce(out, in_, AX.X, op), r, w)

    def memset(self, eng, ap, val, w):
        return self.op(eng, lambda e: e.memset(ap, val), (), w)

    def asel(self, out, in_, pattern, cmp, fill, base, cm, r, w):
        return self.op("pool", lambda e: e.affine_select(out=out, in_=in_, pattern=pattern, compare_op=cmp,
                                                         fill=fill, base=base, channel_multiplier=cm), r, w)


CUT = 10 ** 9
PIPE_W = [1, 1, 1]
M_LIST = None


class Cut(Exception):
    pass


def ck(n):
    if n > CUT:
        raise Cut()


def v3(ap, h):
    return ap.rearrange("p (h d) -> p h d", h=h)


def bc_last(ap2, n):
    p, h = ap2.shape
    return ap2.unsqueeze(2).to_broadcast([p, h, n])


def bc_mid(ap2, n):
    p, d = ap2.shape
    return ap2.unsqueeze(1).to_broadcast([p, n, d])


def phase0(kb, D, S, dbg_tiles=NT):
    kb.begin()
    ident = kb.sb([128, 128], name="ident")
    kb.memset("pool", ident[:], 0.0, [ident])
    kb.asel(ident[:], ident[:], [[-1, 128]], ALU.not_equal, 1.0, 0, 1, [ident], [ident])

    WIN = kb.sb([128, 8, INC], BF16, nslots=8, name="WIN")
    for k in range(8):
        kb.dma("pool", WIN[:, k, :], D["w_in"][k * 128:(k + 1) * 128, :], w=[WIN.b[k]])

    def bload(name, n, q="sp"):
        t = kb.sb([128, n], name="b_" + name)
        kb.dma(q, t[:], D[name][0:1, :].partition_broadcast(128), w=[t])
        return t

    n1w = bload("norm1_w", DM)
    mu = bload("mu", RCOLS)
    w0 = bload("w0", RW)
    a0 = bload("a0", RW)
    kkw = bload("k_k", RW)
    kaw = bload("k_a", RW)
    rkw = bload("r_k", RW)
    lnw = bload("lnx_w", RW)
    lnb = bload("lnx_b", RW)
    wlora = kb.sb([128, RW], BF16, name="wlora")
    kb.dma("pool", wlora[0:64, :], D["w_lora_up"][:, :], w=[wlora])
    kb.dma("pool", wlora[64:128, :], D["a_lora_up"][:, :], w=[wlora])
    glora = kb.sb([128, RW], BF16, name="glora")
    kb.dma("pool", glora[:], D["g_lora_up"][:, :], w=[glora])

    def mask(free, pattern, cm, cmp, init, fill, name):
        t = kb.sb([128] + free, name=name)
        kb.memset("pool", t[0:64], init, [t])
        kb.asel(t[0:64], t[0:64], pattern, cmp, fill, 0, cm, [t], [t])
        kb.dma("sp", t[64:128], t[0:64], r=[t], w=[t])
        return t

    SU = mask([8, 64], [[0, 8], [1, 64]], -1, ALU.is_gt, 1.0, 0.0, "SU")
    IU = mask([8, 64], [[0, 8], [1, 64]], -1, ALU.is_ge, 1.0, 0.0, "IU")
    SL = mask([8, 64], [[0, 8], [-1, 64]], 1, ALU.is_gt, 1.0, 0.0, "SL")
    EYE = mask([8, 64], [[0, 8], [-1, 64]], 1, ALU.not_equal, 0.0, 1.0, "EYE")
    LIN = mask([64], [[1, 64]], -1, ALU.is_ge, 1.0, 0.0, "LIN")
    ONES = kb.sb([128, 64], name="ONES")
    kb.memset("pool", ONES[:], 1.0, [ONES])
    LBD = kb.sb([128, 128], name="LBD")
    kb.memset("pool", LBD[:], 1.0, [LBD])
    kb.asel(LBD[:], LBD[:], [[1, 128]], ALU.is_ge, 0.0, 0, -1, [LBD], [LBD])
    kb.memset("pool", LBD[0:64, 64:128], 0.0, [LBD])
    OBD = kb.sb([128, 128], name="OBD")
    kb.memset("pool", OBD[:], 0.0, [OBD])
    kb.memset("pool", OBD[0:64, 0:64], 1.0, [OBD])
    kb.memset("pool", OBD[64:128, 64:128], 1.0, [OBD])
    NEGH = kb.sb([128, 8], name="NEGH")
    kb.memset("pool", NEGH[:], -0.5, [NEGH])

    posi = kb.sb([128, NT], I32, name="posi")
    kb.dma("sp", posi[:], D["pos"][:, :], w=[posi])
    posf = kb.sb([128, NT], name="posf")
    kb.cp("dve", posf[:], posi[:], [posi], [posf])
    invf = kb.sb([128, 32], name="invf")
    kb.dma("sp", invf[:], D["invf"][:, :], w=[invf])
    X = [kb.sb([128, DM], name=f"x{j}") for j in range(2)]
    xn = kb.sb([128, DM], name="xn")
    COS = kb.sb([128, NT, 32], name="COS")
    SIN = kb.sb([128, NT, 32], name="SIN")
    ang = T(v3(X[0][:], NT), name="ang")
    ang.b = X[0].b
    rf = T(v3(X[1][:], NT), name="rf")
    rf.b = X[1].b
    ri = T(v3(xn[:].bitcast(I32), NT), name="ri")
    ri.b = xn.b
    kb.tt("dve", ang[:], bc_last(posf[:], 32), bc_mid(invf[:], NT), ALU.mult, [posf, invf], [ang])
    for tab, off in ((SIN, 0.0), (COS, 0.25)):
        kb.ts("dve", rf[:], ang[:], 1.0 / (2 * math.pi), off, ALU.mult, ALU.add, [ang], [rf])
        kb.cp("dve", ri[:], rf[:], [rf], [ri])
        kb.cp("dve", tab[:], ri[:], [ri], [tab])
        kb.tt("dve", rf[:], rf[:], tab[:], ALU.subtract, [rf, tab], [rf])
        kb.act(tab[:], rf[:], AF.Sin, [rf], [tab], scale=2 * math.pi)

    ssq = kb.sb([128, 1], name="ssq")
    rstd = kb.sb([128, 1], name="rstd")
    xnT = kb.sb([128, 8, 129], BF16, name="xnT")
    yr = kb.sb([128, RCOLS], name="yr")
    YL2 = [kb.sb([128, RCOLS], name=f"yl{j}") for j in range(2)]
    ro = kb.sb([128, 14, 64], name="ro")
    rt = [kb.sb([128, 8, 32], name=f"rt{j}") for j in range(2)]
    vt = kb.sb([128, 408], name="vt")
    stq = [kb.sb([128, 4, 128], BF16, name="stq0")] * 2
    stk = [kb.sb([128, 4, 128], BF16, name="stk0")] * 2
    stv = [kb.sb([128, 2, 2, 65], BF16, name=f"stv{j}") for j in range(2)]
    stg = [kb.sb([128, 24], name=f"stg{j}") for j in range(2)]
    for j in range(2):
        kb.memset("pool", stv[j][:], 1.0, [stv[j]])
    lo = kb.sb([128, 256], name="lo")
    loT = kb.sb([128, 2, 128], BF16, name="loT")
    W = [kb.sb([128, RW], name=f"w{j}") for j in range(9)]
    sig, a_t, kkn, e3, e4, e5, e6, o_t, pe5 = W
    g_t = kb.sb([128, RW], name="g_t")
    kp = kb.sb([128, RW], name="kp")
    BG2 = [kb.sb([128, RW], name=f"bg{j}") for j in range(2)]
    GW2 = [kb.sb([128, RW], name=f"gw{j}") for j in range(2)]
    sm = [kb.sb([128, 8], name=f"sm{j}") for j in range(4)]
    Bt2 = [kb.sb([128, RW], BF16, name=f"Bt{j}") for j in range(2)]
    Kt2 = [kb.sb([128, RW], BF16, name=f"Kt{j}") for j in range(2)]
    Vb2 = [kb.sb([128, RW], BF16, name=f"Vb{j}") for j in range(2)]
    FM2 = [{q: [kb.sb([128, 8, 64], BF16, name=f"fm{q}{c}_{j}") for c in range(2)] for q in ("AR" if j else "ARBK")}
           for j in range(2)]
    FM2[1]["B"] = FM2[0]["B"]
    FM2[1]["K"] = FM2[0]["K"]
    gC2 = [[kb.sb([128, 8], name=f"gC{c}_{j}") for c in range(2)] for j in range(2)]
    CH_DT = BF16
    Y = [kb.sb([128, 8, 64], CH_DT, name=f"Y{j}") for j in range(2)]
    YT = [kb.sb([128, 8, 64], CH_DT, name=f"YT{j}") for j in range(2)]
    Xi = kb.sb([128, 8, 64], CH_DT, name="Xi")
    Xb2 = [kb.sb([128, 8, 64], BF16, name=f"Xb{j}") for j in range(2)]
    Mkb2 = [kb.sb([128, 8, 64], BF16, name=f"Mkb{j}") for j in range(2)]
    Nbr2 = [kb.sb([128, 8, 64], BF16, name=f"Nbr{j}") for j in range(2)]
    Nkr2 = [kb.sb([128, 8, 64], BF16, name=f"Nkr{j}") for j in range(2)]
    RH = kb.sb([128, 8, 64], BF16, name="RH")
    Ub = kb.sb([128, 8, 64], BF16, name="Ub")
    St = kb.sb([128, 8, 64], name="St")
    Sb = kb.sb([128, 8, 64], BF16, name="Sb")
    kb.memset("pool", St[:], 0.0, [St])
    kb.memset("pool", Sb[:], 0.0, [Sb])
    kb.memset("pool", xnT[:], 0.0, [xnT])

    GR = [(0, 512), (512, 1024), (1024, 1536), (1536, 1792), (1792, 2304), (2304, 2688), (2688, 3096)]

    kb.set_pools({"A": [0, 1], "R": [2, 3, 4, 5], "S": [6], "O": [7]})

    def stageA(i):
        cur, prv = i % 2, (i + 1) % 2
        xt = X[cur]
        kb.dma("sp", xt[:], D["x"][i * 128:(i + 1) * 128, :], w=[xt])
        kb.act(xn[:], xt[:], AF.Square, [xt], [xn, ssq], accum_out=ssq[:])
        kb.ts("dve", rstd[:], ssq[:], 1.0 / DM, RMS_EPS, ALU.mult, ALU.add, [ssq], [rstd])
        kb.tt("pool", rstd[:], rstd[:], NEGH[:, 0:1], ALU.pow, [rstd, NEGH], [rstd])
        kb.stt(xn[:], xt[:], rstd[:, 0:1], n1w[:], ALU.mult, ALU.mult, [xt, rstd, n1w], [xn])
        yield
        for half in range(2):
            ps = kb.psn("A")
            for kk_ in range(4):
                k = half * 4 + kk_
                kb.tr(ps[:, kk_ * 128:(kk_ + 1) * 128], xn[:, k * 128:(k + 1) * 128], ident[:], [xn, ident], [ps])
            if half == 0 and i > 0:
                kb.cp("dve", xnT[:, :, 0:1], xnT[:, :, 128:129], [xnT], [xnT])
            kb.cp("act" if half == 0 else "dve", xnT[:, half * 4:(half + 1) * 4, 1:129], v3(ps[:, :], 4), [ps], [xnT])
        yield
        yl = YL2[i % 2]
        for gi, (ca, cb) in enumerate(GR):
            ps = kb.psn("A")
            n = cb - ca
            for k in range(8):
                kb.mm(ps[:, 0:n], xnT[:, k, 1:129], WIN[:, k, ca:cb], k == 0, k == 7, [xnT, WIN.b[k]], [ps])
            if gi < 4:
                kb.cp("act", yr[:, ca:cb], ps[:, 0:n], [ps], [yr])
                ps2 = kb.psn("A")
                for k in range(8):
                    kb.mm(ps2[:, 0:n], xnT[:, k, 0:128], WIN[:, k, ca:cb], k == 0, k == 7, [xnT, WIN.b[k]], [ps2])
                kb.tt("dve", yl[:, ca:cb], ps2[:, 0:n], yr[:, ca:cb], ALU.subtract, [ps2, yr], [yl])
                kb.tt("pool", yl[:, ca:cb], yl[:, ca:cb], mu[:, ca:cb], ALU.mult, [yl, mu], [yl])
                kb.tt("pool", yl[:, ca:cb], yl[:, ca:cb], yr[:, ca:cb], ALU.add, [yl, yr], [yl])
            elif gi in (4, 5):
                nh = 8 if gi == 4 else 6
                h0 = 0 if gi == 4 else 8
                pv = v3(ps[:, 0:nh * 64], nh)
                x1, x2 = pv[:, :, 0:32], pv[:, :, 32:64]
                cb_ = bc_mid(COS[:, i, :], nh)
                sb_ = bc_mid(SIN[:, i, :], nh)
                t1, t2 = rt[0][:, 0:nh, :], rt[1][:, 0:nh, :]
                kb.tt("dve", t1, x1, cb_, ALU.mult, [ps, COS], [rt[0]])
                kb.tt("dve", t2, x2, sb_, ALU.mult, [ps, SIN], [rt[1]])
                kb.tt("pool", ro[:, h0:h0 + nh, 0:32], t1, t2, ALU.subtract, [rt[0], rt[1]], [ro])
                kb.tt("dve", t1, x2, cb_, ALU.mult, [ps, COS], [rt[0]])
                kb.tt("dve", t2, x1, sb_, ALU.mult, [ps, SIN], [rt[1]])
                kb.tt("pool", ro[:, h0:h0 + nh, 32:64], t1, t2, ALU.add, [rt[0], rt[1]], [ro])
            else:
                kb.cp("act", vt[:], ps[:, 0:408], [ps], [vt])
            yield
        yield
        rof = ro[:].rearrange("p h d -> p (h d)")
        ps = kb.psn("A")
        for p_ in range(4):
            kb.tr(ps[:, p_ * 128:(p_ + 1) * 128], rof[:, p_ * 128:(p_ + 1) * 128], ident[:], [ro, ident], [ps])
        kb.cp("act", stq[cur][:], v3(ps[:, :], 4), [ps], [stq[cur]])
        kb.dma("sp", S["dQT"][:, :, i * 128:(i + 1) * 128], stq[cur][:], r=[stq[cur]], w=[S["dQT"]])
        ps = kb.psn("A")
        for p_ in range(3):
            kb.tr(ps[:, p_ * 128:(p_ + 1) * 128], rof[:, (4 + p_) * 128:(5 + p_) * 128], ident[:], [ro, ident], [ps])
        kb.tr(ps[:, 384:512], vt[:, 0:128], ident[:], [vt, ident], [ps])
        kb.cp("dve", stk[cur][:], v3(ps[:, :], 4), [ps], [stk[cur]])
        kb.dma("sp", S["dKT"][:, :, i * 128:(i + 1) * 128], stk[cur][:], r=[stk[cur]], w=[S["dKT"]])
        kb.cp("pool", stv[cur][:, :, :, 0:64], vt[:, 128:384].rearrange("p (a g d) -> p a g d", a=2, g=2), [vt], [stv[cur]])
        kb.dma("sp", S["dV"][:, i, :, :, :], stv[cur][:], r=[stv[cur]], w=[S["dV"]])
        kb.act(stg[cur][:], vt[:, 384:408], AF.Tanh, [vt], [stg[cur]], scale=0.5)
        kb.act(stg[cur][:], stg[cur][:], AF.Identity, [stg[cur]], [stg[cur]], scale=0.5, bias=0.5)
        kb.dma("sp", S["dG"][:, i, :], stg[cur][:], r=[stg[cur]], w=[S["dG"]])


    def bind(i):
        j = i % 2
        return (BG2[j], GW2[j], Bt2[j], Kt2[j], Vb2[j], FM2[j], gC2[j], Xb2[j], Mkb2[j], Nbr2[j], Nkr2[j])

    def stageR(i):
        cur, prv = i % 2, (i + 1) % 2
        yl = YL2[i % 2]
        BG, GW, Bt, Kt, Vb, FM, gC, Xb, Mkb, Nbr, Nkr = bind(i)
        r_ = yl[:, 0:512]
        k_ = yl[:, 512:1024]
        v_ = yl[:, 1024:1536]
        yield
        kb.act(lo[:, 0:64], yl[:, 1536:1600], AF.Tanh, [yl], [lo])
        kb.cp("pool", lo[:, 64:128], yl[:, 1600:1664], [yl], [lo])
        kb.act(lo[:, 128:256], yl[:, 1664:1792], AF.Tanh, [yl], [lo], scale=0.5)
        kb.act(lo[:, 128:256], lo[:, 128:256], AF.Identity, [lo], [lo], scale=0.5, bias=0.5)
        kb.tt("pool", kkn[:], k_, kkw[:], ALU.mult, [yl, kkw], [kkn])
        kb.act(e3[:], kkn[:], AF.Square, [kkn], [e3])
        kb.red("dve", sm[0][:], v3(e3[:], 8), ALU.add, [e3], [sm[0]])
        kb.ts("dve", sm[0][:], sm[0][:], 1e-12, None, ALU.max, None, [sm[0]], [sm[0]])
        kb.tt("pool", sm[0][:], sm[0][:], NEGH[:], ALU.pow, [sm[0], NEGH], [sm[0]])
        kb.tt("dve", v3(kkn[:], 8), v3(kkn[:], 8), bc_last(sm[0][:], 64), ALU.mult, [kkn, sm[0]], [kkn])
        ps = kb.psn("R")
        for j in range(2):
            kb.tr(ps[:, j * 128:(j + 1) * 128], lo[:, j * 128:(j + 1) * 128], ident[:], [lo, ident], [ps])
        kb.cp("act", loT[:], v3(ps[:, 0:256], 2), [ps], [loT])
        psw = kb.psn("R")
        kb.mm(psw[:, :], loT[0:64, 0, :], wlora[0:64, :], True, True, [loT, wlora], [psw])
        psa = kb.psn("R")
        kb.mm(psa[:, :], loT[64:128, 0, :], wlora[64:128, :], True, True, [loT, wlora], [psa])
        psg = kb.psn("R")
        kb.mm(psg[:, :], loT[:, 1, :], glora[:, :], True, True, [loT, glora], [psg])
        kb.tt("dve", sig[:], psw[:, :], w0[:], ALU.add, [psw, w0], [sig])
        kb.act(sig[:], sig[:], AF.Tanh, [sig], [sig], scale=0.5)
        kb.ts("dve", sig[:], sig[:], 0.5, 0.5, ALU.mult, ALU.add, [sig], [sig])
        kb.tt("dve", a_t[:], psa[:, :], a0[:], ALU.add, [psa, a0], [a_t])
        kb.act(a_t[:], a_t[:], AF.Tanh, [a_t], [a_t], scale=0.5)
        kb.act(a_t[:], a_t[:], AF.Identity, [a_t], [a_t], scale=0.5, bias=0.5)
        kb.cp("act", g_t[:], psg[:, :], [psg], [g_t])
        yield
        kb.stt(kp[:], a_t[:], -1.0, kaw[:], ALU.add, ALU.mult, [a_t, kaw], [kp])
        kb.stt(kp[:], kp[:], 1.0, k_, ALU.add, ALU.mult, [kp, yl], [kp])
        kb.tt("pool", a_t[:], a_t[:], kkn[:], ALU.mult, [a_t, kkn], [a_t])
        beta = a_t
        yield
        psc = kb.psn("R")
        pst = kb.psn("R")
        kb.mm(psc[:, :], LBD[:, :], sig[:, :], True, True, [LBD, sig], [psc])
        kb.mm(pst[:, :], OBD[:, :], sig[:, :], True, True, [OBD, sig], [pst])
        kb.act(e3[:], psc[:, :], AF.Exp, [psc], [e3], scale=CDEC)
        kb.act(e4[:], psc[:, :], AF.Exp, [psc], [e4], scale=-CDEC)
        kb.act(e5[:], pst[:, :], AF.Exp, [pst], [e5], scale=CDEC)
        kb.act(e6[:], sig[:], AF.Exp, [sig], [e6], scale=-CDEC)
        yield
        for c in range(2):
            rows = slice(64 * c, 64 * c + 64)
            ps = kb.psn("R")
            for hf in range(2):
                for h in range(8):
                    kb.mm(ps[64 * hf:64 * hf + 64, 2 * h:2 * h + 2], sig[rows, h * 64:(h + 1) * 64], ONES[rows, 0:2],
                          True, True, [sig, ONES], [ps])
            kb.act(gC[c][:].unsqueeze(2), v3(ps[:, 0:16], 8)[:, :, 0:1], AF.Exp, [ps], [gC[c]], scale=CDEC)
        kb.tt("dve", e6[:], e6[:], e3[:], ALU.mult, [e6, e3], [e6])
        kb.stt(kkn[:], kkn[:], -1.0, e6[:], ALU.mult, ALU.mult, [kkn, e6], [kkn])
        A_tok = kkn
        kb.tt("pool", e3[:], e3[:], r_, ALU.mult, [e3, yl], [e3])
        R_tok = e3
        kb.tt("dve", e5[:], e5[:], e4[:], ALU.mult, [e5, e4], [e5])
        kb.tt("pool", e6[:], beta[:], e4[:], ALU.mult, [beta, e4], [e6])
        B_tok = e6
        kb.tt("dve", Bt[:], beta[:], e5[:], ALU.mult, [beta, e5], [Bt])
        kb.tt("dve", Kt[:], kp[:], e5[:], ALU.mult, [kp, e5], [Kt])
        kb.tt("pool", e4[:], e4[:], kp[:], ALU.mult, [e4, kp], [e4])
        K_tok = e4
        kb.cp("act", Vb[:], v_, [yl], [Vb])
        yield
        for q, src in (("A", A_tok), ("R", R_tok), ("B", B_tok), ("K", K_tok)):
            for c in range(2):
                rows = slice(64 * c, 64 * c + 64)
                ps = kb.psn("R")
                for h in range(8):
                    kb.tr(ps[0:64, h * 64:(h + 1) * 64], src[rows, h * 64:(h + 1) * 64], ident[rows, rows], [src, ident], [ps])
                kb.cp("act" if c == 0 else "dve", FM[q][c][0:64], v3(ps[0:64, :], 8), [ps], [FM[q][c]])
                if c == 1 and q in "AR":
                    kb.dma("sp", FM[q][c][64:128], FM[q][c][0:64], r=[FM[q][c]], w=[FM[q][c]])

        yield
        def gram(lq, rq):
            ps = kb.psn("R")
            for c in range(2):
                rows = slice(64 * c, 64 * c + 64)
                for h in range(8):
                    kb.mm(ps[rows, h * 64:(h + 1) * 64], FM[lq][c][0:64, h, :], FM[rq][c][0:64, h, :], True, True,
                          [FM[lq][c], FM[rq][c]], [ps])
            return ps
        ps = gram("B", "A")
        kb.tt("dve", Y[0][:], v3(ps[:, :], 8), SU[:], ALU.mult, [ps, SU], [Y[0]])
        ps = gram("A", "B")
        kb.tt("dve", YT[0][:], v3(ps[:, :], 8), SL[:], ALU.mult, [ps, SL], [YT[0]])
        ps = gram("K", "A")
        kb.tt("dve", Mkb[:], v3(ps[:, :], 8), SU[:], ALU.mult, [ps, SU], [Mkb])
        ps = gram("B", "R")
        kb.tt("dve", Nbr[:], v3(ps[:, :], 8), IU[:], ALU.mult, [ps, IU], [Nbr])
        ps = gram("K", "R")
        kb.tt("dve", Nkr[:], v3(ps[:, :], 8), IU[:], ALU.mult, [ps, IU], [Nkr])
        kb.tt("pool", Xi[:], Y[0][:], EYE[:], ALU.add, [Y[0], EYE], [Xi])

        yield
        def mm3(la, ra):
            ps = kb.psn("R")
            for c in range(2):
                rows = slice(64 * c, 64 * c + 64)
                for h in range(8):
                    kb.mm(ps[rows, h * 64:(h + 1) * 64], la[rows, h, :], ra[rows, h, :], True, True, [la, ra], [ps])
            return ps
        for lv in range(1, 6):
            po, pn = (lv - 1) % 2, lv % 2
            psy = mm3(YT[po], Y[po]) if lv < 5 else None
            pyt = mm3(Y[po], YT[po])
            psx = mm3(YT[po], Xi) if lv >= 2 else None
            if psy is not None:
                kb.cp("act", Y[pn][:], v3(psy[:, :], 8), [psy], [Y[pn]])
            kb.cp("act", YT[pn][:], v3(pyt[:, :], 8), [pyt], [YT[pn]])
            if psx is not None:
                kb.tt("dve", Xi[:], Xi[:], v3(psx[:, :], 8), ALU.add, [Xi, psx], [Xi])
            yield
        psx = mm3(YT[1], Xi)
        kb.tt("dve", Xb[:], Xi[:], v3(psx[:, :], 8), ALU.add, [Xi, psx], [Xb])
        yield
        kb.tt("pool", e5[:], r_, kp[:], ALU.mult, [yl, kp], [e5])
        kb.tt("pool", e5[:], e5[:], rkw[:], ALU.mult, [e5, rkw], [e5])
        kb.red("dve", sm[3][:], v3(e5[:], 8), ALU.add, [e5], [sm[3]])
        kb.tt("dve", v3(BG[:], 8), v3(v_, 8), bc_last(sm[3][:], 64), ALU.mult, [yl, sm[3]], [BG])
        kb.tt("pool", BG[:], BG[:], lnb[:], ALU.add, [BG, lnb], [BG])
        kb.tt("pool", BG[:], BG[:], g_t[:], ALU.mult, [BG, g_t], [BG])
        kb.tt("pool", GW[:], g_t[:], lnw[:], ALU.mult, [g_t, lnw], [GW])


    def stageS(i):
        BG, GW, Bt, Kt, Vb, FM, gC, Xb, Mkb, Nbr, Nkr = bind(i)
        pso = kb.psn("O")
        for c in range(2):
            rows = slice(64 * c, 64 * c + 64)
            ps1 = kb.psn("S")
            for h in range(8):
                hs = slice(h * 64, (h + 1) * 64)
                kb.mm(ps1[rows, hs], Mkb[rows, h, :], Vb[rows, hs], True, False, [Mkb, Vb], [ps1])
                kb.mm(ps1[rows, hs], FM["A"][c][rows, h, :], Sb[rows, h, :], False, True, [FM["A"][c], Sb], [ps1])
            kb.cp("act", RH[rows], v3(ps1[rows, :], 8), [ps1], [RH])
            yield
            ps2 = kb.psn("S")
            for h in range(8):
                hs = slice(h * 64, (h + 1) * 64)
                kb.mm(ps2[rows, hs], Xb[rows, h, :], RH[rows, h, :], True, True, [Xb, RH], [ps2])
            kb.cp("dve", Ub[rows], v3(ps2[rows, :], 8), [ps2], [Ub])
            yield
            for h in range(8):
                hs = slice(h * 64, (h + 1) * 64)
                kb.mm(pso[rows, hs], FM["R"][c][rows, h, :], Sb[rows, h, :], True, False, [FM["R"][c], Sb], [pso])
                kb.mm(pso[rows, hs], Nbr[rows, h, :], Ub[rows, h, :], False, False, [Nbr, Ub], [pso])
                kb.mm(pso[rows, hs], Nkr[rows, h, :], Vb[rows, hs], False, True, [Nkr, Vb], [pso])
            ps4 = kb.psn("S")
            for hf in range(2):
                orow = slice(64 * hf, 64 * hf + 64)
                for h in range(8):
                    hs = slice(h * 64, (h + 1) * 64)
                    kb.mm(ps4[orow, hs], Bt[rows, hs], Ub[rows, h, :], True, False, [Bt, Ub], [ps4])
                    kb.mm(ps4[orow, hs], Kt[rows, hs], Vb[rows, hs], False, True, [Kt, Vb], [ps4])
            kb.tt("dve", St[:], St[:], bc_last(gC[c][:], 64), ALU.mult, [St, gC[c]], [St])
            kb.tt("dve", St[:], St[:], v3(ps4[:, :], 8), ALU.add, [St, ps4], [St])
            kb.cp("act", Sb[:], St[:], [St], [Sb])
            yield

        yield
        kb.cp("act", o_t[:], pso[:, :], [pso], [o_t])
        o3 = v3(o_t[:], 8)
        kb.red("dve", sm[1][:], o3, ALU.add, [o_t], [sm[1]])
        kb.ts("dve", sm[1][:], sm[1][:], 1.0 / 64, None, ALU.mult, None, [sm[1]], [sm[1]])
        kb.tt("dve", o3, o3, bc_last(sm[1][:], 64), ALU.subtract, [o_t, sm[1]], [o_t])
        kb.act(pe5[:], o_t[:], AF.Square, [o_t], [pe5])
        kb.red("dve", sm[2][:], v3(pe5[:], 8), ALU.add, [pe5], [sm[2]])
        kb.ts("dve", sm[2][:], sm[2][:], 1.0 / 64, LNX_EPS, ALU.mult, ALU.add, [sm[2]], [sm[2]])
        kb.tt("pool", sm[2][:], sm[2][:], NEGH[:], ALU.pow, [sm[2], NEGH], [sm[2]])
        kb.tt("dve", o3, o3, bc_last(sm[2][:], 64), ALU.mult, [o_t, sm[2]], [o_t])
        yield
        kb.tt("pool", o_t[:], o_t[:], GW[:], ALU.mult, [o_t, GW], [o_t])
        kb.tt("pool", o_t[:], o_t[:], BG[:], ALU.add, [o_t, BG], [o_t])
        kb.dma("sp", S["dOR"][i * 128:(i + 1) * 128, :], o_t[:], r=[o_t], w=[S["dOR"]])

    def interleave(gens, weights=None):
        gens = list(gens)
        weights = list(weights) if weights else [1] * len(gens)
        while gens:
            for g_, w_ in list(zip(gens, weights)):
                for _ in range(w_):
                    try:
                        next(g_)
                    except StopIteration:
                        j_ = gens.index(g_)
                        gens.pop(j_)
                        weights.pop(j_)
                        break

    interleave([stageA(0)])
    interleave([stageR(0)] + ([stageA(1)] if dbg_tiles > 1 else []))
    for i in range(dbg_tiles):
        gens = [stageS(i)]
        wts = [PIPE_W[0]]
        if i + 1 < dbg_tiles:
            gens.append(stageR(i + 1))
            wts.append(PIPE_W[1])
        if i + 2 < dbg_tiles:
            gens.append(stageA(i + 2))
            wts.append(PIPE_W[2])
        interleave(gens, wts)
    kb.end()


def declare(nc):
    D = {}

    def din(name, shape, dt=F32):
        D[name] = nc.dram_tensor(name, list(shape), dt, kind="ExternalInput").ap()

    din("x", [T_SEQ, DM])
    din("pos", [128, NT], I32)
    din("invf", [128, 32])
    din("norm1_w", [1, DM])
    din("w_in", [DM, INC])
    din("mu", [1, RCOLS])
    for n in ("w0", "a0", "k_k", "k_a", "r_k", "lnx_w", "lnx_b"):
        din(n, [1, RW])
    din("w_lora_up", [64, RW])
    din("a_lora_up", [64, RW])
    din("g_lora_up", [128, RW])
    din("cmp_pos_k", [32, 64])
    din("cmp_pos_v", [32, 64])
    din("cmp_k_w1", [2048, 256])
    din("cmp_k_w2", [256, 64])
    din("cmp_v_w1", [2048, 256])
    din("cmp_v_w2", [256, 64])
    din("w_out", [DM, DM])
    din("norm2_w", [1, DM])
    din("ffn_w1", [DM, DFF])
    din("ffn_w3", [DM, DFF])
    din("ffn_w2", [DFF, DM])
    din("final_norm_w", [1, DM])
    return D


def build(stage=3, dbg_tiles=NT):
    nc = bass.Bass("TRN2", target_bir_lowering=False)
    D = declare(nc)
    kb = KB(nc)
    dbg = stage < 3
    kind = "ExternalOutput" if dbg else "Internal"
    S = {
        "dQT": kb.dram("dQT", [128, 4, T_SEQ], BF16, kind),
        "dKT": kb.dram("dKT", [128, 4, T_SEQ], BF16, kind),
        "dV": kb.dram("dV", [128, NT, 2, 2, 65], BF16, kind),
        "dG": kb.dram("dG", [128, NT, 24], F32, kind),
        "dOR": kb.dram("dOR", [T_SEQ, RW], F32, kind),
    }
    S["dKC"] = kb.dram("dKC", [128, 256], BF16, kind)
    S["dVC"] = kb.dram("dVC", [128, 2, 2, 129], BF16, kind)
    phase0(kb, D, S, dbg_tiles)
    S["dH1"] = kb.dram("dH1", [T_SEQ, DM], F32, kind)
    if stage >= 2:
        phase1a(kb, D, S)
    if stage >= 2.5:
        phase1b(kb, D, S, M_LIST)
    if stage >= 3:
        out_ap = nc.dram_tensor("out", [T_SEQ, DM], F32, kind="ExternalOutput").ap()
        phase2(kb, D, S, out_ap)
    kb.gs.close()
    return nc


def host_inputs(inputs):
    g = lambda k: np.ascontiguousarray(np.asarray(inputs[k])[0], dtype=np.float32)
    w_in = g("w_in")
    nsa0 = RCOLS
    qperm = []
    for p in range(4):
        qperm += list(range(nsa0 + p * 64, nsa0 + p * 64 + 64))
        qperm += list(range(nsa0 + (p + 4) * 64, nsa0 + (p + 4) * 64 + 64))
    o = nsa0 + 512
    kc, vc, ks, vs, kw, vw = (list(range(o + j * 128, o + (j + 1) * 128)) for j in range(6))
    gl = list(range(o + 768, o + 792))
    perm = list(range(RCOLS)) + qperm + kc + ks + kw + vc + vs + vw + gl
    assert len(perm) == INC
    w_in_p = np.ascontiguousarray(w_in[:, perm])
    invf = (10000.0 ** (-np.arange(32, dtype=np.float32) / 32)).astype(np.float32)
    common = {
        "invf": np.ascontiguousarray(np.broadcast_to(invf[None, :], (128, 32))),
        "norm1_w": g("norm1_w")[None, :],
        "w_in": w_in_p,
        "mu": g("mu_rwkv")[None, :],
        "w0": g("w0")[None, :], "a0": g("a0")[None, :], "k_k": g("k_k")[None, :], "k_a": g("k_a")[None, :],
        "r_k": g("r_k").reshape(1, RW), "lnx_w": g("lnx_w")[None, :], "lnx_b": g("lnx_b")[None, :],
        "w_lora_up": g("w_lora_up"), "a_lora_up": g("a_lora_up"), "g_lora_up": g("g_lora_up"),
        "cmp_pos_k": g("cmp_pos_k"), "cmp_pos_v": g("cmp_pos_v"),
        "cmp_k_w1": g("cmp_k_w1"), "cmp_k_w2": g("cmp_k_w2"), "cmp_v_w1": g("cmp_v_w1"), "cmp_v_w2": g("cmp_v_w2"),
        "w_out": g("w_out"), "norm2_w": g("norm2_w")[None, :],
        "ffn_w1": g("ffn_w1"), "ffn_w3": g("ffn_w3"), "ffn_w2": g("ffn_w2"),
        "final_norm_w": np.asarray(inputs["final_norm_w"], dtype=np.float32).reshape(1, DM),
    }
    x = np.asarray(inputs["x"], dtype=np.float32)
    pos = np.asarray(inputs["positions"]).astype(np.int32)
    maps = []
    for b in range(x.shape[0]):
        m = dict(common)
        m["x"] = np.ascontiguousarray(x[b])
        m["pos"] = np.ascontiguousarray(pos[b].reshape(NT, 128).T)
        maps.append(m)
    return maps


def kernel(**inputs):
    nc = build(3)
    maps = host_inputs(inputs)
    res = run_bass_kernel_spmd(nc, maps, core_ids=list(range(len(maps))))
    return np.stack([np.asarray(r["out"]) for r in res.results], axis=0).astype(np.float32)


def phase1a(kb, D, S):
    kb.begin()
    ident = kb.sb([128, 128], name="ident")
    kb.memset("pool", ident[:], 0.0, [ident])
    kb.asel(ident[:], ident[:], [[-1, 128]], ALU.not_equal, 1.0, 0, 1, [ident], [ident])
    KT = kb.sb([128, 2, T_SEQ], BF16, nslots=2, name="KTc")
    kb.dma("sp", KT[:, 0, :], S["dKT"][:, 0, :], r=[S["dKT"]], w=[KT.b[0]])
    kb.dma("sp", KT[:, 1, :], S["dKT"][:, 3, :], r=[S["dKT"]], w=[KT.b[1]])
    W1 = {}
    W2 = {}
    PT = {}
    for z, nm in (("k", "cmp_k"), ("v", "cmp_v")):
        w1 = kb.sb([128, 32, 256], BF16, name="w1" + z)
        src = D[nm + "_w1"].rearrange("(l d) h -> d l h", d=64)
        for hf in range(2):
            for lq in range(4):
                kb.dma("pool", w1[64 * hf:64 * hf + 64, lq * 8:(lq + 1) * 8, :], src[:, lq * 8:(lq + 1) * 8, :], w=[w1])
        W1[z] = w1
        w2 = kb.sb([128, 2, 64], BF16, name="w2" + z)
        kb.dma("pool", w2[:], D[nm + "_w2"].rearrange("(c p) d -> p c d", p=128), w=[w2])
        W2[z] = w2
        pos = kb.sb([32, 64], name="pos" + z)
        kb.dma("sp", pos[:], D["cmp_pos_" + z][:, :], w=[pos])
        ps = kb.psn()
        kb.tr(ps[0:64, 0:32], pos[:, :], ident[0:32, 0:32], [pos, ident], [ps])
        pT = kb.sb([64, 32, 2], BF16, name="posT" + z)
        for j in range(2):
            kb.cp("dve", pT[:, :, j:j + 1], ps[0:64, 0:32].unsqueeze(2), [ps], [pT])
        PT[z] = pT
    OV = kb.sb([128, 2, 64], name="OV")
    kb.memset("pool", OV[:], 1.0, [OV])
    for nt in range(2):
        kb.asel(OV[:, nt, :], OV[:, nt, :], [[-4, 64]], ALU.is_ge, 0.0, 128 * nt + 1, 1, [OV], [OV])
        kb.asel(OV[:, nt, :], OV[:, nt, :], [[4, 64]], ALU.is_ge, 0.0, 3 - 128 * nt, -1, [OV], [OV])
    VC = kb.sb([128, 2, 2, 129], BF16, name="VC")
    kb.memset("pool", VC[:], 1.0, [VC])
    for nt in range(2):
        for g in range(2):
            kb.cp("pool", VC[:, nt, g, 65:129], OV[:, nt, :], [OV], [VC])
    KCT = kb.sb([128, 256], BF16, name="KCT")
    kb.memset("pool", KCT[:], 0.0, [KCT])
    bias = kb.sb([128, 1], name="bias")
    hb = kb.sb([128, 255], name="hb")
    t1 = kb.sb([128, 255], name="gt1")
    t2 = kb.sb([128, 255], name="gt2")
    GH = kb.sb([128, 2, 256], BF16, name="GH")
    for zi, z in enumerate(("k", "v")):
        for g in range(2):
            rows = slice(64 * g, 64 * g + 64)
            for hc in range(2):
                hcs = slice(hc * 128, (hc + 1) * 128)
                psb = kb.psn()
                for l in range(32):
                    kb.mm(psb[:, 0:2], W1[z][0:64, l, hcs], PT[z][:, l, :], l == 0, l == 31, [W1[z], PT[z]], [psb])
                kb.cp("dve", bias[:], psb[:, 0:1], [psb], [bias])
                ps = kb.psn()
                for l in range(32):
                    kb.mm(ps[:, 0:255], W1[z][rows, l, hcs], KT[rows, zi, l:l + 4065:16], l == 0, l == 31,
                          [W1[z], KT.b[zi]], [ps])
                kb.act(hb[:], ps[:, 0:255], AF.Identity, [ps, bias], [hb], bias=bias[:, 0:1])
                kb.tt("dve", t1[:], hb[:], hb[:], ALU.mult, [hb], [t1])
                kb.ts("dve", t1[:], t1[:], 0.044715, 1.0, ALU.mult, ALU.add, [t1], [t1])
                kb.tt("dve", t1[:], t1[:], hb[:], ALU.mult, [t1, hb], [t1])
                kb.act(t2[:], t1[:], AF.Tanh, [t1], [t2], scale=0.7978845608028654)
                kb.stt(t1[:], t2[:], 1.0, hb[:], ALU.add, ALU.mult, [t2, hb], [t1])
                kb.ts("dve", GH[:, hc, 0:255], t1[:], 0.5, None, ALU.mult, None, [t1], [GH])
            if z == "k":
                ps = kb.psn()
                for hc in range(2):
                    kb.mm(ps[rows, 0:255], W2[z][:, hc, :], GH[:, hc, 0:255], hc == 0, hc == 1, [W2[z], GH], [ps])
                kb.cp("act", KCT[rows, 0:255], ps[rows, 0:255], [ps], [KCT])
            else:
                for nt in range(2):
                    kn = 128 if nt == 0 else 127
                    ps = kb.psn()
                    for hc in range(2):
                        kb.mm(ps[0:kn, 0:64], GH[:, hc, nt * 128:nt * 128 + kn], W2[z][:, hc, :], hc == 0, hc == 1,
                              [GH, W2[z]], [ps])
                    kb.cp("act", VC[0:kn, nt, g, 0:64], ps[0:kn, 0:64], [ps], [VC])
    kb.dma("sp", S["dKC"][:, :], KCT[:], r=[KCT], w=[S["dKC"]])
    kb.dma("sp", S["dVC"][:, :, :, :], VC[:], r=[VC], w=[S["dVC"]])
    kb.end()


def phase1b(kb, D, S, m_list=None):
    kb.begin()
    ident = kb.sb([128, 128], name="ident")
    kb.memset("pool", ident[:], 0.0, [ident])
    kb.asel(ident[:], ident[:], [[-1, 128]], ALU.not_equal, 1.0, 0, 1, [ident], [ident])
    qT = kb.sb([128, 4, T_SEQ], BF16, name="qT")
    for p in range(4):
        kb.dma("sp", qT[:, p, :], S["dQT"][:, p, :], r=[S["dQT"]], w=[qT])
    KT = kb.sb([128, 2, 2, T_SEQ], BF16, name="KT")
    for z in range(2):
        kb.memset("pool", KT[64:128, z, 0, :], 0.0, [KT])
        kb.memset("pool", KT[0:64, z, 1, :], 0.0, [KT])
    for z, slot in ((0, 1), (1, 2)):
        kb.dma("sp", KT[0:64, z, 0, :], S["dKT"][0:64, slot, :], r=[S["dKT"]], w=[KT])
        kb.dma("sp", KT[64:128, z, 1, :], S["dKT"][64:128, slot, :], r=[S["dKT"]], w=[KT])
    VA = kb.sb([128, NT, 2, 2, 65], BF16, name="VA")
    for q4 in range(4):
        kb.dma("sp", VA[:, q4 * 8:(q4 + 1) * 8], S["dV"][:, q4 * 8:(q4 + 1) * 8], r=[S["dV"]], w=[VA])
    G = kb.sb([128, NT, 24], name="G")
    kb.dma("sp", G[:], S["dG"][:, :, :], r=[S["dG"]], w=[G])
    KCT = kb.sb([128, 256], BF16, name="KCT")
    kb.dma("sp", KCT[:], S["dKC"][:, :], r=[S["dKC"]], w=[KCT])
    VC = kb.sb([128, 2, 2, 129], BF16, name="VC")
    kb.dma("sp", VC[:], S["dVC"][:, :, :, :], r=[S["dVC"]], w=[VC])
    WOUT = kb.sb([128, 8, DM], BF16, nslots=8, name="WOUT")
    for k in range(8):
        kb.dma("pool", WOUT[:, k, :], D["w_out"][k * 128:(k + 1) * 128, :], w=[WOUT.b[k]])
    Eexp = kb.sb([128, T_SEQ], BF16, name="Eexp")
    kb.memset("pool", Eexp[0:64], 1.0, [Eexp])
    kb.asel(Eexp[0:64], Eexp[0:64], [[1, T_SEQ]], ALU.is_ge, 0.0, 0, -64, [Eexp], [Eexp])
    kb.asel(Eexp[0:64], Eexp[0:64], [[-1, T_SEQ]], ALU.is_ge, 0.0, 63, 64, [Eexp], [Eexp])
    kb.dma("sp", Eexp[64:128], Eexp[0:64], r=[Eexp], w=[Eexp])
    Wb = kb.sb([128, 126], name="Wb")
    for hi in range(2):
        rows = slice(64 * hi, 64 * hi + 64)
        kb.memset("pool", Wb[rows, 0:61 + hi], 0.0, [Wb])
        kb.memset("pool", Wb[rows, 61 + hi:63 + hi], 1e4, [Wb])
        kb.memset("pool", Wb[rows, 63 + hi:126], -1e30, [Wb])

    Pc = [kb.sb([128, 4, 128], BF16, name=f"Pc{j}") for j in range(2)]
    PTs2 = [kb.sb([128, NT, 512], BF16, nslots=NT, name=f"PTs{j}") for j in range(2)]
    PTw2 = [kb.sb([128, 5, 512], BF16, nslots=5, name=f"PTw{j}") for j in range(2)]
    rz = [kb.sb([128, 4], name=f"rz{j}") for j in range(2)]
    cf = [kb.sb([128, 4], name=f"cf{j}") for j in range(2)]
    t3 = [kb.sb([128, 4, 64], name=f"t3{j}") for j in range(2)]
    imp = kb.sb([128, 64], name="imp")
    imp2 = kb.sb([128, 64], name="imp2")
    m8 = kb.sb([128, 8], name="m8")
    m8b = kb.sb([128, 8], name="m8b")
    sel = kb.sb([128, 2, 64], name="sel")
    selT2 = [kb.sb([128, 4, 128], BF16, name=f"selT{j}") for j in range(2)]
    ON2 = [kb.sb([128, 512], name=f"ON{j}") for j in range(2)]
    ORt = [kb.sb([128, 512], name="ORt0")] * 2
    XT = [kb.sb([128, DM], name="xt0")] * 2
    oT = kb.sb([128, 8, 128], BF16, name="oT")
    h1 = kb.sb([128, DM], name="h1")
    kb.set_pools({"C": [0, 1], "S": [2, 3, 4, 5], "V": [6, 7]})

    def finish_branch(psO, g, m, br, first, st):
        ON = ON2[m % 2]
        O3 = v3(psO[:, 0:260], 4)
        kb.ts("dve", rz[st][:].unsqueeze(2), O3[:, :, 64:65], 1e-30, None, ALU.add, None, [psO], [rz[st]])
        kb.op("dve", lambda e: e.reciprocal(rz[st][:], rz[st][:]), [rz[st]], [rz[st]])
        gate3 = v3(G[:, m, :], 8)
        kb.tt("dve", cf[st][:].unsqueeze(2), gate3[:, 4 * g:4 * g + 4, br:br + 1], rz[st][:].unsqueeze(2), ALU.mult,
              [G, rz[st]], [cf[st]])
        ONg = v3(ON[:, g * 256:(g + 1) * 256], 4)
        if first:
            kb.tt("dve", ONg, O3[:, :, 0:64], bc_last(cf[st][:], 64), ALU.mult, [psO, cf[st]], [ON])
        else:
            kb.tt("dve", t3[st][:], O3[:, :, 0:64], bc_last(cf[st][:], 64), ALU.mult, [psO, cf[st]], [t3[st]])
            kb.tt("pool", ONg, ONg, t3[st][:], ALU.add, [ON, t3[st]], [ON])

    def stageC(m, g, it):
        qs = slice(m * 128, (m + 1) * 128)
        grow = slice(64 * g, 64 * g + 64)
        selT = selT2[it % 2]
        nts = [0] + ([1] if m >= 16 else [])
        for nt in nts:
            kn = 128 if nt == 0 else 127
            ps = kb.psn("C")
            kb.mm(ps[0:kn, :], KCT[grow, nt * 128:nt * 128 + kn], qT[grow, :, qs], True, True, [KCT, qT], [ps])
            kb.act(Pc[nt][0:kn], v3(ps[0:kn, :], 4), AF.Exp, [ps], [Pc[nt]], scale=0.125)
            if not (nt == 0 and m >= 17):
                kb.asel(Pc[nt][0:kn], Pc[nt][0:kn], [[0, 4], [1, 128]], ALU.is_ge, 0.0,
                        128 * m - 2048 * nt - 31, -16, [Pc[nt]], [Pc[nt]])
        yield
        psA = kb.psn("C")
        psB = kb.psn("C")
        for h in range(4):
            for ii, nt in enumerate(nts):
                kn = 128 if nt == 0 else 127
                kb.mm(psA[:, h * 65:(h + 1) * 65], Pc[nt][0:kn, h, :], VC[0:kn, nt, g, 0:65], ii == 0,
                      ii == len(nts) - 1, [Pc[nt], VC], [psA])
        for h in range(4):
            for ii, nt in enumerate(nts):
                kn = 128 if nt == 0 else 127
                kb.mm(psB[:, h * 64:(h + 1) * 64], Pc[nt][0:kn, h, :], VC[0:kn, nt, g, 65:129], ii == 0,
                      ii == len(nts) - 1, [Pc[nt], VC], [psB])
        finish_branch(psA, g, m, 0, True, 0)
        yield
        kb.tt("dve", t3[0][:], v3(psB[:, 0:256], 4), bc_last(rz[0][:], 64), ALU.mult, [psB, rz[0]], [t3[0]])
        kb.red("dve", imp[:], t3[0][:].rearrange("p h j -> p j h"), ALU.add, [t3[0]], [imp])
        kb.tt("dve", imp[:], imp[:], Wb[:, 62 - 2 * m:126 - 2 * m], ALU.add, [imp, Wb], [imp])
        if m >= 1:
            kb.ts("dve", imp[:, 0:1], imp[:, 0:1], 1e4, None, ALU.add, None, [imp], [imp])
        yield
        kb.op("dve", lambda e: e.max(m8[:], imp[:]), [imp], [m8])
        kb.op("dve", lambda e: e.match_replace(imp2[:], m8[:], imp[:], -1e30), [imp, m8], [imp2])
        kb.op("dve", lambda e: e.max(m8b[:], imp2[:]), [imp2], [m8b])
        kb.ts("dve", sel[:], imp[:].unsqueeze(1).to_broadcast([128, 2, 64]), m8b[:, 7:8], -800.0, ALU.is_lt, ALU.mult,
              [imp, m8b], [sel])
        yield
        ps = kb.psn("C")
        kb.tr(ps[:, 0:128], sel[:].rearrange("p a j -> p (a j)"), ident[:], [sel, ident], [ps])
        kb.cp("act", selT[:], ps[:, 0:128].unsqueeze(1).to_broadcast([128, 4, 128]), [ps], [selT])

    def stageS(m, g, it):
        qs = slice(m * 128, (m + 1) * 128)
        selT = selT2[it % 2]
        PTs, PTw = PTs2[it % 2], PTw2[it % 2]
        if g == 1:
            kb.dma("sp", ORt[0][:], S["dOR"][qs, :], r=[S["dOR"]], w=[ORt[0]])
            kb.dma("sp", XT[0][:], D["x"][qs, :], w=[XT[0]])
        kt0 = max(0, m - 4)
        for kt in range(kt0, m + 1):
            ksl = slice(kt * 128, (kt + 1) * 128)
            sl = kt - kt0
            psS = kb.psn("S")
            kb.mm(psS[:, :], KT[:, 1, g, ksl], qT[:, :, qs], True, True, [KT, qT], [psS])
            kb.act(PTw[:, sl, :], psS[:, :], AF.Exp, [psS], [PTw.b[sl]], scale=0.125)
            if kt == m:
                kb.asel(v3(PTw[:, sl, :], 4), v3(PTw[:, sl, :], 4), [[0, 4], [1, 128]], ALU.is_ge, 0.0, 0, -1,
                        [PTw.b[sl]], [PTw.b[sl]])
            if kt == m - 4:
                kb.asel(v3(PTw[:, sl, :], 4), v3(PTw[:, sl, :], 4), [[0, 4], [-1, 128]], ALU.is_ge, 0.0, -1, 1,
                        [PTw.b[sl]], [PTw.b[sl]])
            yield
        for kt in range(m + 1):
            ksl = slice(kt * 128, (kt + 1) * 128)
            psS = kb.psn("S")
            kb.mm(psS[:, :], KT[:, 0, g, ksl], qT[:, :, qs], True, False, [KT, qT], [psS])
            kb.mm(psS[:, :], Eexp[:, ksl], selT[:, :, :], False, True, [Eexp, selT], [psS])
            kb.act(PTs[:, kt, :], psS[:, :], AF.Exp, [psS], [PTs.b[kt]], scale=0.125)
            if kt == m:
                kb.asel(v3(PTs[:, kt, :], 4), v3(PTs[:, kt, :], 4), [[0, 4], [1, 128]], ALU.is_ge, 0.0, 0, -1,
                        [PTs.b[kt]], [PTs.b[kt]])
            yield

    def stageV(m, g, it):
        qs = slice(m * 128, (m + 1) * 128)
        PTs, PTw = PTs2[it % 2], PTw2[it % 2]
        ON = ON2[m % 2]
        cur = m % 2
        kt0 = max(0, m - 4)
        psO = kb.psn("V")
        for h in range(4):
            for kt in range(kt0, m + 1):
                sl = kt - kt0
                kb.mm(psO[:, h * 65:(h + 1) * 65], PTw[:, sl, h * 128:(h + 1) * 128], VA[:, kt, 1, g, :],
                      kt == kt0, kt == m, [PTw.b[sl], VA], [psO])
        finish_branch(psO, g, m, 2, False, 1)
        yield
        psO = kb.psn("V")
        for h in range(4):
            for kt in range(m + 1):
                kb.mm(psO[:, h * 65:(h + 1) * 65], PTs[:, kt, h * 128:(h + 1) * 128], VA[:, kt, 0, g, :],
                      kt == 0, kt == m, [PTs.b[kt], VA], [psO])
            yield
        finish_branch(psO, g, m, 1, False, 1)
        yield
        if g == 0:
            return
        for half in range(2):
            ps = kb.psn("V")
            for kk_ in range(4):
                src = ORt[cur] if half == 0 else ON
                kb.tr(ps[:, kk_ * 128:(kk_ + 1) * 128], src[:, kk_ * 128:(kk_ + 1) * 128], ident[:], [src, ident], [ps])
            kb.cp("act", oT[:, half * 4:(half + 1) * 4, :], v3(ps[:, :], 4), [ps], [oT])
        yield
        for half in range(2):
            ps = kb.psn("V")
            for k in range(8):
                kb.mm(ps[:, :], oT[:, k, :], WOUT[:, k, half * 512:(half + 1) * 512], k == 0, k == 7, [oT, WOUT.b[k]], [ps])
            kb.tt("dve", h1[:, half * 512:(half + 1) * 512], XT[cur][:, half * 512:(half + 1) * 512], ps[:, :], ALU.add,
                  [XT[cur], ps], [h1])
        kb.dma("pool", S["dH1"][qs, :], h1[:], r=[h1], w=[S["dH1"]])

    def interleave(gens):
        gens = list(gens)
        while gens:
            for g_ in list(gens):
                try:
                    next(g_)
                except StopIteration:
                    gens.remove(g_)

    its = [(m, g) for m in (m_list if m_list is not None else range(NT)) for g in range(2)]
    N_ = len(its)
    interleave([stageC(its[0][0], its[0][1], 0)])
    interleave([stageS(its[0][0], its[0][1], 0)] + ([stageC(its[1][0], its[1][1], 1)] if N_ > 1 else []))
    for n_ in range(N_):
        gens = [stageV(its[n_][0], its[n_][1], n_)]
        if n_ + 1 < N_:
            gens.append(stageS(its[n_ + 1][0], its[n_ + 1][1], n_ + 1))
        if n_ + 2 < N_:
            gens.append(stageC(its[n_ + 2][0], its[n_ + 2][1], n_ + 2))
        interleave(gens)
    kb.end()


def phase2(kb, D, S, out_ap, n_super=8):
    kb.begin()
    ident = kb.sb([128, 128], name="ident")
    kb.memset("pool", ident[:], 0.0, [ident])
    kb.asel(ident[:], ident[:], [[-1, 128]], ALU.not_equal, 1.0, 0, 1, [ident], [ident])
    NF = DFF // 128
    W1 = kb.sb([128, 8, DFF], BF16, nslots=8, name="W1")
    W3 = kb.sb([128, 8, DFF], BF16, nslots=8, name="W3")
    W2 = kb.sb([128, NF, DM], BF16, nslots=NF, name="W2")
    for k in range(8):
        kb.dma("pool", W1[:, k, :], D["ffn_w1"][k * 128:(k + 1) * 128, :], w=[W1.b[k]])
        kb.dma("pool", W3[:, k, :], D["ffn_w3"][k * 128:(k + 1) * 128, :], w=[W3.b[k]])
    for f in range(NF):
        kb.dma("pool", W2[:, f, :], D["ffn_w2"][f * 128:(f + 1) * 128, :], w=[W2.b[f]])
    n2w = kb.sb([128, DM], name="n2w")
    kb.dma("sp", n2w[:], D["norm2_w"][0:1, :].partition_broadcast(128), w=[n2w])
    fnw = kb.sb([128, DM], name="fnw")
    kb.dma("sp", fnw[:], D["final_norm_w"][0:1, :].partition_broadcast(128), w=[fnw])
    H = [kb.sb([128, 4, DM], nslots=4, name="H0")]
    un = kb.sb([128, DM], name="un")
    ssq = kb.sb([128, 1], name="ssq2")
    rstd = kb.sb([128, 1], name="rstd2")
    uT = kb.sb([128, 8, 512], BF16, name="uT")
    hT = kb.sb([128, NF, 512], BF16, nslots=NF, name="hT")
    sg = [kb.sb([128, 512], name=f"sg{j}") for j in range(2)]
    h2 = [kb.sb([128, DM], name="h20")]
    ot = [kb.sb([128, DM], name="ot0")]

    for s_ in range(n_super):
        Hs = H[0]
        for j in range(4):
            r0 = s_ * 512 + j * 128
            kb.dma("sp", Hs[:, j, :], S["dH1"][r0:r0 + 128, :], r=[S["dH1"]], w=[Hs.b[j]])
        for j in range(4):
            kb.act(un[:], Hs[:, j, :], AF.Square, [Hs.b[j]], [un, ssq], accum_out=ssq[:])
            kb.ts("dve", rstd[:], ssq[:], 1.0 / DM, RMS_EPS, ALU.mult, ALU.add, [ssq], [rstd])
            kb.act(rstd[:], rstd[:], AF.Sqrt, [rstd], [rstd])
            kb.op("dve", lambda e: e.reciprocal(rstd[:], rstd[:]), [rstd], [rstd])
            kb.stt(un[:], Hs[:, j, :], rstd[:, 0:1], n2w[:], ALU.mult, ALU.mult, [Hs.b[j], rstd, n2w], [un])
            for half in range(2):
                ps = kb.psn()
                for kk_ in range(4):
                    k = half * 4 + kk_
                    kb.tr(ps[:, kk_ * 128:(kk_ + 1) * 128], un[:, k * 128:(k + 1) * 128], ident[:], [un, ident], [ps])
                kb.cp("act" if half == 0 else "dve", uT[:, half * 4:(half + 1) * 4, j * 128:(j + 1) * 128],
                      v3(ps[:, :], 4), [ps], [uT])
        for f in range(NF):
            fs = slice(f * 128, (f + 1) * 128)
            pa = kb.psn()
            for k in range(8):
                kb.mm(pa[:, :], W1[:, k, fs], uT[:, k, :], k == 0, k == 7, [W1.b[k], uT], [pa])
            pb = kb.psn()
            for k in range(8):
                kb.mm(pb[:, :], W3[:, k, fs], uT[:, k, :], k == 0, k == 7, [W3.b[k], uT], [pb])
            kb.act(sg[f % 2][:], pa[:, :], AF.Silu, [pa], [sg[f % 2]])
            kb.tt("dve", hT[:, f, :], sg[f % 2][:], pb[:, :], ALU.mult, [sg[f % 2], pb], [hT.b[f]])
        for j in range(4):
            r0 = s_ * 512 + j * 128
            hh = h2[0]
            for half in range(2):
                ps = kb.psn()
                for f in range(NF):
                    kb.mm(ps[:, :], hT[:, f, j * 128:(j + 1) * 128], W2[:, f, half * 512:(half + 1) * 512], f == 0,
                          f == NF - 1, [hT.b[f], W2.b[f]], [ps])
                kb.tt("dve", hh[:, half * 512:(half + 1) * 512], Hs[:, j, half * 512:(half + 1) * 512], ps[:, :], ALU.add,
                      [Hs.b[j], ps], [hh])
            oo = ot[0]
            kb.act(oo[:], hh[:], AF.Square, [hh], [oo, ssq], accum_out=ssq[:])
            kb.ts("dve", rstd[:], ssq[:], 1.0 / DM, RMS_EPS, ALU.mult, ALU.add, [ssq], [rstd])
            kb.act(rstd[:], rstd[:], AF.Sqrt, [rstd], [rstd])
            kb.op("dve", lambda e: e.reciprocal(rstd[:], rstd[:]), [rstd], [rstd])
            kb.stt(oo[:], hh[:], rstd[:, 0:1], fnw[:], ALU.mult, ALU.mult, [hh, rstd, fnw], [oo])
            kb.dma("sp", out_ap[r0:r0 + 128, :], oo[:], r=[oo])
    kb.end()
```
